# Optimizing a Trainium2 kernel written in Bass

```python
import numpy as np
import jax, jax.numpy as jnp
from jax import lax

D_MODEL = 1024
BATCH = 2
SEQ = 16384
DEPTH = 4

CHUNK = 64
N_MIXERS = 4
HEAD_DIM = 64
MIX_WIDTH = D_MODEL
MEM_LEN = 256
X_HEADS = 4
X_WIDTH = X_HEADS * HEAD_DIM
BRANCH_WIDTH = MIX_WIDTH + X_WIDTH
SG_BLOCK = 128
SG_GROUPS = 8
SG_GROUP_CH = MIX_WIDTH // SG_GROUPS
SWA_WINDOW = 128
SWA_Q_HEADS = MIX_WIDTH // HEAD_DIM
SWA_KV_HEADS = 2
SWA_GROUP = SWA_Q_HEADS // SWA_KV_HEADS
WINDOW_CHUNKS = SWA_WINDOW // CHUNK
BAND = (WINDOW_CHUNKS + 1) * CHUNK
REL_BUCKETS = 32
REL_MAX_DIST = 128
CONV_WIDTH = 31
SHORT_CONV_WIDTH = 3
D_FF = -(-8 * D_MODEL // (3 * 256)) * 256
DEEPNORM_ALPHA = (2 * DEPTH) ** 0.25
DEEPNORM_BETA = (8 * DEPTH) ** -0.25
LN_EPS = 1e-5
NEG_INF = -1e30
N_PER_TYPE = tuple(len(range(m, DEPTH, N_MIXERS)) for m in range(N_MIXERS))

kernel_name = 'hybrid_chunk_causal_interleaved_trunk'


def layer_norm(x, g, b):
    xf = x.astype(jnp.float32)
    mu = jnp.mean(xf, -1, keepdims=True)
    var = jnp.mean(jnp.square(xf - mu), -1, keepdims=True)
    y = (xf - mu) * lax.rsqrt(var + LN_EPS)
    return (y * g.astype(jnp.float32) + b.astype(jnp.float32)).astype(x.dtype)


def causal_depthwise_conv(x, w):
    k = w.shape[0]
    return lax.conv_general_dilated(
        x, w[:, None, :].astype(x.dtype), window_strides=(1,), padding=[(k - 1, 0)],
        dimension_numbers=('NWC', 'WIO', 'NWC'), feature_group_count=x.shape[-1])


def t5_bucket(rel):
    nb = REL_BUCKETS // 2
    ret = (rel > 0).astype(np.int32) * nb
    n = np.abs(rel)
    max_exact = nb // 2
    large = max_exact + (np.log(np.maximum(n, 1) / max_exact)
                         / np.log(REL_MAX_DIST / max_exact) * (nb - max_exact)).astype(np.int32)
    large = np.minimum(large, nb - 1)
    return (ret + np.where(n < max_exact, n, large)).astype(np.int32)


def spatial_gating_mixer(x, w_in, v_g, v_b, w_s, b_s):
    bsz, s, _ = x.shape
    h = x @ w_in
    z = jax.nn.gelu(h[..., :2 * MIX_WIDTH])
    qx = h[..., 2 * MIX_WIDTH:]
    u = z[..., :MIX_WIDTH]
    v = layer_norm(z[..., MIX_WIDTH:], v_g, v_b)
    pos = np.arange(SG_BLOCK) // CHUNK
    mask = pos[None, :] <= pos[:, None]
    w_m = jnp.where(mask[None], w_s, 0.0)
    v = v.reshape(bsz, s // SG_BLOCK, SG_BLOCK, SG_GROUPS, SG_GROUP_CH)
    sv = jnp.einsum('gij,bnjgc->bnigc', w_m, v) + b_s.T[:, :, None]
    return u * sv.reshape(bsz, s, MIX_WIDTH), qx


def swa_sink_mixer(x, w_in, sinks, rel_bias):
    bsz, s, _ = x.shape
    n_chunks = s // CHUNK
    kv_w = SWA_KV_HEADS * HEAD_DIM
    h = x @ w_in
    q = h[..., :MIX_WIDTH].reshape(bsz, n_chunks, CHUNK, SWA_KV_HEADS, SWA_GROUP, HEAD_DIM)
    k = h[..., MIX_WIDTH:MIX_WIDTH + kv_w]
    v = h[..., MIX_WIDTH + kv_w:MIX_WIDTH + 2 * kv_w]
    qx = h[..., MIX_WIDTH + 2 * kv_w:]

    def band(t):
        t = jnp.pad(t, ((0, 0), (WINDOW_CHUNKS * CHUNK, 0), (0, 0)))
        t = t.reshape(bsz, n_chunks + WINDOW_CHUNKS, CHUNK, SWA_KV_HEADS, HEAD_DIM)
        return jnp.concatenate([t[:, w:w + n_chunks] for w in range(WINDOW_CHUNKS + 1)], axis=2)

    kb, vb = band(k), band(v)
    scores = jnp.einsum('bnqkgd,bnskd->bnkgqs', q, kb).astype(jnp.float32) * (HEAD_DIM ** -0.5)
    qpos = np.arange(CHUNK)[:, None]
    kpos = np.arange(BAND)[None, :] - WINDOW_CHUNKS * CHUNK
    buckets = t5_bucket(kpos - qpos)
    bias = rel_bias.astype(jnp.float32)[buckets]
    bias = bias.transpose(2, 0, 1).reshape(SWA_KV_HEADS, SWA_GROUP, CHUNK, BAND)
    valid = (np.arange(n_chunks)[:, None] + np.arange(BAND)[None, :] // CHUNK - WINDOW_CHUNKS) >= 0
    scores = jnp.where(valid[None, :, None, None, None, :], scores + bias, NEG_INF)
    sink = jnp.broadcast_to(sinks.astype(jnp.float32).reshape(SWA_KV_HEADS, SWA_GROUP, 1, 1),
                            scores.shape[:-1] + (1,))
    probs = jax.nn.softmax(jnp.concatenate([scores, sink], axis=-1), axis=-1)[..., :BAND]
    out = jnp.einsum('bnkgqs,bnskd->bnqkgd', probs.astype(x.dtype), vb)
    return out.reshape(bsz, s, MIX_WIDTH), qx


def conformer_conv_mixer(x, w_in, conv_w, conv_b, ln_g, ln_b):
    h = x @ w_in
    a, g = h[..., :MIX_WIDTH], h[..., MIX_WIDTH:2 * MIX_WIDTH]
    qx = h[..., 2 * MIX_WIDTH:]
    y = causal_depthwise_conv(a * jax.nn.sigmoid(g), conv_w) + conv_b
    return jax.nn.silu(layer_norm(y, ln_g, ln_b)), qx


def short_conv_mixer(x, w_in, conv_w):
    h = x @ w_in
    bg = h[..., :MIX_WIDTH]
    cg = h[..., MIX_WIDTH:2 * MIX_WIDTH]
    hv = h[..., 2 * MIX_WIDTH:3 * MIX_WIDTH]
    qx = h[..., 3 * MIX_WIDTH:]
    return bg * causal_depthwise_conv(cg * hv, conv_w), qx


def memory_cross_attention(qx, mem, w_mem_kv):
    bsz, s, _ = qx.shape
    q = qx.reshape(bsz, s, X_HEADS, HEAD_DIM)
    kv = (mem @ w_mem_kv).reshape(bsz, mem.shape[1], 2, X_HEADS, HEAD_DIM)
    sc = jnp.einsum('bshd,bmhd->bhsm', q, kv[:, :, 0]).astype(jnp.float32) * (HEAD_DIM ** -0.5)
    p = jax.nn.softmax(sc, axis=-1).astype(qx.dtype)
    return jnp.einsum('bhsm,bmhd->bshd', p, kv[:, :, 1]).reshape(bsz, s, X_WIDTH)


def swiglu_ffn(x, w_in, w_down):
    h = x @ w_in
    return (jax.nn.silu(h[..., :D_FF]) * h[..., D_FF:]) @ w_down


def setup_inputs(seed: int = 0) -> dict:
    key = jax.random.key(seed)
    ks = iter(jax.random.split(key, 48))

    def nrm(shape, scale):
        return jax.random.normal(next(ks), shape, jnp.float32) * scale

    n_a, n_b, n_c, n_d = N_PER_TYPE
    d = D_MODEL
    return {
        'x': nrm((BATCH, SEQ, d), 1.0),
        'mem': nrm((BATCH, MEM_LEN, d), 1.0),
        'a_w_in': nrm((n_a, d, 2 * MIX_WIDTH + X_WIDTH), d ** -0.5),
        'a_v_ln_g': 1.0 + nrm((n_a, MIX_WIDTH), 0.02),
        'a_v_ln_b': nrm((n_a, MIX_WIDTH), 0.02),
        'a_w_s': nrm((n_a, SG_GROUPS, SG_BLOCK, SG_BLOCK), SG_BLOCK ** -0.5),
        'a_b_s': 1.0 + nrm((n_a, SG_GROUPS, SG_BLOCK), 0.02),
        'b_w_in': nrm((n_b, d, MIX_WIDTH + 2 * SWA_KV_HEADS * HEAD_DIM + X_WIDTH), d ** -0.5),
        'b_sinks': nrm((n_b, SWA_Q_HEADS), 0.5),
        'rel_bias': nrm((REL_BUCKETS, SWA_Q_HEADS), 0.1),
        'c_w_in': nrm((n_c, d, 2 * MIX_WIDTH + X_WIDTH), d ** -0.5),
        'c_conv_w': nrm((n_c, CONV_WIDTH, MIX_WIDTH), CONV_WIDTH ** -0.5),
        'c_conv_b': nrm((n_c, MIX_WIDTH), 0.02),
        'c_ln_g': 1.0 + nrm((n_c, MIX_WIDTH), 0.02),
        'c_ln_b': nrm((n_c, MIX_WIDTH), 0.02),
        'd_w_in': nrm((n_d, d, 3 * MIX_WIDTH + X_WIDTH), d ** -0.5),
        'd_conv_w': nrm((n_d, SHORT_CONV_WIDTH, MIX_WIDTH), SHORT_CONV_WIDTH ** -0.5),
        'w_mem_kv': nrm((DEPTH, d, 2 * X_WIDTH), d ** -0.5),
        'w_o': nrm((DEPTH, BRANCH_WIDTH, d), BRANCH_WIDTH ** -0.5 * DEEPNORM_BETA),
        'ln1_g': 1.0 + nrm((DEPTH, d), 0.02),
        'ln1_b': nrm((DEPTH, d), 0.02),
        'ffn_w_in': nrm((DEPTH, d, 2 * D_FF), d ** -0.5),
        'ffn_w_down': nrm((DEPTH, D_FF, d), D_FF ** -0.5 * DEEPNORM_BETA),
        'ln2_g': 1.0 + nrm((DEPTH, d), 0.02),
        'ln2_b': nrm((DEPTH, d), 0.02),
    }


def reference(x, mem, a_w_in, a_v_ln_g, a_v_ln_b, a_w_s, a_b_s, b_w_in, b_sinks, rel_bias,
              c_w_in, c_conv_w, c_conv_b, c_ln_g, c_ln_b, d_w_in, d_conv_w, w_mem_kv, w_o,
              ln1_g, ln1_b, ffn_w_in, ffn_w_down, ln2_g, ln2_b):
    for i in range(DEPTH):
        m, j = i % N_MIXERS, i // N_MIXERS
        if m == 0:
            mix, qx = spatial_gating_mixer(x, a_w_in[j], a_v_ln_g[j], a_v_ln_b[j], a_w_s[j], a_b_s[j])
        elif m == 1:
            mix, qx = swa_sink_mixer(x, b_w_in[j], b_sinks[j], rel_bias)
        elif m == 2:
            mix, qx = conformer_conv_mixer(x, c_w_in[j], c_conv_w[j], c_conv_b[j], c_ln_g[j], c_ln_b[j])
        else:
            mix, qx = short_conv_mixer(x, d_w_in[j], d_conv_w[j])
        branch = jnp.concatenate([mix, memory_cross_attention(qx, mem, w_mem_kv[i])], axis=-1)
        x = layer_norm(DEEPNORM_ALPHA * x + branch @ w_o[i], ln1_g[i], ln1_b[i])
        x = layer_norm(DEEPNORM_ALPHA * x + swiglu_ffn(x, ffn_w_in[i], ffn_w_down[i]), ln2_g[i], ln2_b[i])
    return x
```

```python
import numpy as np
from contextlib import ExitStack
import concourse.bass as bass
import concourse.mybir as mybir
from concourse.bass_utils import run_bass_kernel_spmd

F32 = mybir.dt.float32
BF16 = mybir.dt.bfloat16
AF = mybir.ActivationFunctionType
ALU = mybir.AluOpType

D = 1024
DFF = 2816
HALO = 256
TW = 512
NMAIN = 8
NCORES = 8
ALPHA = float(8.0 ** 0.25)
EPS = 1e-5
NSLOT = 3
SLOT_ELEMS = 22 * 256

W_IN_COLS = [2304, 1536, 2304, 3328]
W_IN_NAMES = ["a_w_in", "b_w_in", "c_w_in", "d_w_in"]


def _param_layout():
    off = {}
    cur = 0

    def add(name, n):
        nonlocal cur
        off[name] = cur
        cur += n

    for l in range(4):
        for which in ("ln1", "ln2"):
            add(f"{which}_g{l}", 8)
            add(f"{which}_b{l}", 8)
    add("avln_g", 8)
    add("avln_b", 8)
    add("cln_g", 8)
    add("cln_b", 8)
    add("cconv_b", 8)
    add("sinks", 8)
    add("cconv_w", 8 * 31)
    add("dconv_w", 8 * 3)
    add("eps", 1)
    return off, cur


POFF, NPAR = _param_layout()


def _fm(v):
    return np.ascontiguousarray(np.asarray(v, np.float32).reshape(8, 128).T)


def _t5_bucket(rel):
    nb = 16
    ret = (rel > 0).astype(np.int32) * nb
    n = np.abs(rel)
    max_exact = nb // 2
    large = max_exact + (np.log(np.maximum(n, 1) / max_exact)
                         / np.log(128 / max_exact) * (nb - max_exact)).astype(np.int32)
    large = np.minimum(large, nb - 1)
    return (ret + np.where(n < max_exact, n, large)).astype(np.int32)


class Prog:
    ENGS = ["pe", "act", "dve", "pool", "sp"]

    def __init__(self, nc):
        self.nc = nc
        self.ops = {e: [] for e in self.ENGS}
        self.track = {}
        self.dma_cnt = {}
        self.signal = set()
        self.seen = {e: {} for e in self.ENGS}
        self.waitall_keys = set()

    @staticmethod
    def box(ap):
        a = ap.ap
        off = ap.offset
        name = ap.tensor.name
        if "DRAM" in str(ap.space).upper():
            ext = sum((c - 1) * abs(st) for st, c in a) + 1
            return (name, 0, 1, off, off + ext, True)
        row = a[0][0]
        npart = a[0][1]
        sz = mybir.dt.size(ap.dtype)
        p0 = off // row if row > 0 else 0
        f0 = (off - p0 * row) * sz
        ext = (sum((c - 1) * abs(st) for st, c in a[1:]) + 1) * sz
        return (name, p0, p0 + npart, f0, f0 + ext, False)

    @staticmethod
    def _ov(a, b):
        return a[1] < b[2] and b[1] < a[2] and a[3] < b[4] and b[3] < a[4]

    @staticmethod
    def _cov(a, b):
        return a[1] <= b[1] and a[2] >= b[2] and a[3] <= b[3] and a[4] >= b[4]

    def add(self, eng, fn, reads=(), writes=(), dma=None, extra_deps=()):
        idx = len(self.ops[eng])
        deps = set(extra_deps)
        rb = [self.box(ap) for ap in reads]
        wb = [self.box(ap) for ap in writes]
        for b in rb:
            ents = self.track.get(b[0])
            if ents:
                for k, e in ents.items():
                    if e["w"] is not None and self._ov(k, b):
                        deps.add(e["w"])
        for b in wb:
            ents = self.track.get(b[0])
            if ents:
                for k in list(ents.keys()):
                    e = ents[k]
                    if self._ov(k, b):
                        if e["w"] is not None:
                            deps.add(e["w"])
                        deps.update(e["r"].values())
                        deps.update(e["rd"])
                        if k != b and self._cov(b, k):
                            del ents[k]
        if dma is not None:
            cnt = self.dma_cnt.get(dma, 0) + 16
            self.dma_cnt[dma] = cnt
            token = ("dma", dma, cnt)
        else:
            token = (eng, idx)
        for b in rb:
            if b[5]:
                continue
            ents = self.track.setdefault(b[0], {})
            e = ents.get(b)
            if e is None:
                e = {"w": None, "r": {}, "rd": []}
                ents[b] = e
            if dma is not None:
                e["rd"].append(token)
            else:
                e["r"][eng] = token
        for b in wb:
            self.track.setdefault(b[0], {})[b] = {"w": token, "r": {}, "rd": []}
        best = {}
        seen = self.seen[eng]
        for d in deps:
            if d[0] == "dma":
                k = ("dma", d[1])
                if d[1] in self.waitall_keys:
                    if seen.get(k, 0) < (1 << 60):
                        seen[k] = 1 << 60
                        best[k] = None
                    continue
                if seen.get(k, 0) >= d[2]:
                    continue
                if k not in best or (best[k] is not None and best[k] < d[2]):
                    best[k] = d[2]
            else:
                if d[0] == "pe" and eng == "pe":
                    continue
                if seen.get(d[0], -1) >= d[1]:
                    continue
                if d[0] not in best or best[d[0]] < d[1]:
                    best[d[0]] = d[1]
        waits = []
        for k, v in best.items():
            if isinstance(k, tuple):
                if v is not None:
                    seen[k] = max(seen.get(k, 0), v)
                waits.append(("dma", k[1], v))
            else:
                seen[k] = v
                waits.append((k, v))
                self.signal.add((k, v))
        self.ops[eng].append({"fn": fn, "waits": waits, "dma": dma})
        return token

    def mm(self, out, lhsT, rhs, start=True, stop=True):
        return self.add("pe", lambda h: h.matmul(out, lhsT=lhsT, rhs=rhs, start=start, stop=stop),
                        [lhsT, rhs], [out])

    def act(self, out, in_, func, scale=None, bias=None):
        reads = [in_]
        kw = {}
        if scale is not None:
            kw["scale"] = scale
            if not isinstance(scale, (int, float)):
                reads.append(scale)
        if bias is not None:
            kw["bias"] = bias
            if not isinstance(bias, (int, float)):
                reads.append(bias)
        return self.add("act", lambda h: h.activation(out, in_, func, **kw), reads, [out])

    def tt(self, eng, out, in0, in1, op):
        return self.add(eng, lambda h: h.tensor_tensor(out, in0, in1, op), [in0, in1], [out])

    def ts(self, eng, out, in0, s1, s2, op0, op1=None):
        reads = [in0] + [s for s in (s1, s2) if s is not None and not isinstance(s, (int, float))]
        if op1 is None:
            return self.add(eng, lambda h: h.tensor_scalar(out, in0, s1, None, op0), reads, [out])
        return self.add(eng, lambda h: h.tensor_scalar(out, in0, s1, s2, op0, op1), reads, [out])

    def stt(self, out, in0, scalar, in1, op0, op1):
        reads = [in0, in1] + ([] if isinstance(scalar, (int, float)) else [scalar])
        return self.add("dve", lambda h: h.scalar_tensor_tensor(out, in0, scalar, in1, op0, op1), reads, [out])

    def copy(self, eng, out, in_):
        if eng == "act":
            return self.act(out, in_, AF.Copy)
        return self.add(eng, lambda h: h.tensor_copy(out, in_), [in_], [out])

    def recip(self, out, in_):
        return self.add("dve", lambda h: h.reciprocal(out, in_), [in_], [out])

    def memset(self, eng, ap, val):
        return self.add(eng, lambda h: h.memset(ap, val), [], [ap])

    def dma(self, eng, out, in_, key):
        return self.add(eng, lambda h: h.dma_start(out=out, in_=in_), [in_], [out], dma=key)

    def emit(self, block, engsem, dmasem):
        sigcount = {}
        for e in self.ENGS:
            c = 0
            for i in range(len(self.ops[e])):
                if (e, i) in self.signal:
                    c += 1
                    sigcount[(e, i)] = c
        handles = {"pe": block.tensor, "act": block.scalar, "dve": block.vector,
                   "pool": block.gpsimd, "sp": block.sync}

        def make(e):
            ops = self.ops[e]

            def body(h):
                for i, op in enumerate(ops):
                    for wt in op["waits"]:
                        if wt[0] == "dma":
                            cnt = wt[2] if wt[2] is not None else self.dma_cnt[wt[1]]
                            h.wait_ge(dmasem[wt[1]], cnt)
                        else:
                            h.wait_ge(engsem[wt[0]], sigcount[wt])
                    if op["fn"] is None:
                        continue
                    ins = op["fn"](h)
                    if op["dma"] is not None:
                        ins.then_inc(dmasem[op["dma"]], 16)
                    elif (e, i) in self.signal:
                        ins.then_inc(engsem[e], 1)
            return body

        for e in self.ENGS:
            handles[e](make(e))


def build(n_main=NMAIN):
    nc = bass.Bass("TRN2", target_bir_lowering=False)
    TTOK = HALO + n_main * TW
    P = Prog(nc)
    P.waitall_keys.add("setup")

    def din(name, shape, dt=F32):
        return nc.dram_tensor(name, list(shape), dt, kind="ExternalInput").ap()

    xT_d = din("xT", [D, TTOK])
    memT_d = din("memT", [D, 256])
    flag_d = din("flag", [128, 1])
    par_d = din("params", [128, NPAR])
    bsrep_d = din("bs_rep", [128, 8 * 128])
    wsT_d = din("wsT", [128, 8 * 128])
    biasT_d = din("biasT", [128, 2 * 16 * 128])
    ident_d = din("ident", [128, 128])
    win_d = [din(W_IN_NAMES[l], [D, W_IN_COLS[l]]) for l in range(4)]
    wkv_d = din("w_mem_kv", [4 * D, 512])
    wo_d = din("w_o", [4 * 1280, D])
    fin_d = din("ffn_w_in", [4 * D, 2 * DFF])
    fdn_d = din("ffn_w_down", [4 * DFF, D])
    outT_d = nc.dram_tensor("outT", [D, n_main * TW], F32, kind="ExternalOutput").ap()

    def dscr(name, shape):
        return nc.dram_tensor(name, list(shape), BF16, kind="Internal").ap()

    win_s = [dscr(f"s_win{l}", [D, W_IN_COLS[l]]) for l in range(4)]
    wkv_s = dscr("s_wkv", [4 * D, 512])
    wo_s = dscr("s_wo", [4 * 1280, D])
    fin_s = dscr("s_fin", [4 * D, 2 * DFF])
    fdn_s = dscr("s_fdn", [4 * DFF, D])

    es = ExitStack()
    with es:
        def sb(name, shape, dt=F32):
            return es.enter_context(nc.sbuf_tensor(name, list(shape), dt))

        xT = sb("xT_sb", [128, 8, TW])
        xb = sb("xb_sb", [128, 8, TW], BF16)
        xT_h = sb("xT_h", [128, 8, HALO])
        xb_h = sb("xb_h", [128, 8, HALO], BF16)
        brT = sb("brT", [128, 10, TW], BF16)
        zT = sb("zT", [128, 8, TW])
        tmpf = sb("tmpf", [128, 6, TW])
        tmpb = sb("tmpb", [128, 4, TW], BF16)
        slots = [sb(f"wslot{i}", [128, SLOT_ELEMS], BF16) for i in range(NSLOT)]
        RF = sb("regf", [128, 6144])
        RB = sb("regb", [128, 11264], BF16)
        Kdup = sb("Kdup", [128, 2, 128 + TW], BF16)
        Vpad = sb("Vpad", [128, 5, 2, 2, 128], BF16)
        gcar = sb("gcar", [128, 8, 32], BF16)
        pcar = sb("pcar", [128, 8, 2], BF16)
        memK = sb("memK", [128, 4, 2, 256], BF16)
        memV = sb("memV", [128, 4, 2, 4, 128], BF16)
        biasTb = sb("biasTb", [128, 2, 16, 128], BF16)
        wmT = sb("wmT", [128, 8, 128], BF16)
        Cg = sb("Cg", [128, 8, 128])
        par = sb("par", [128, NPAR])
        flag = sb("flag_sb", [128, 1])
        esk = sb("esk", [128, 8])
        identb = sb("identb", [128, 128], BF16)
        onesb = sb("onesb", [128, 128], BF16)
        onesp = sb("onesp", [128, 2, 128], BF16)
        small = sb("small", [128, 64])
        negone = sb("negone", [1, 128], BF16)
        mxr = sb("mxr", [1, TW], BF16)
        ps = [es.enter_context(nc.psum_tensor(f"ps{i}", [128, 512], F32)) for i in range(8)]
        wsT = RF[:, 0:1024].rearrange("p (a b) -> p a b", a=8)
        bsrep = RF[:, 1024:2048].rearrange("p (a b) -> p a b", a=8)
        memTb = RB[:, 0:2048].rearrange("p (a b) -> p a b", a=8)
        diag3 = RB[:, 4608:7680].rearrange("p (c k i) -> p c k i", c=8, k=3)
        DG = [RF[:, i * 1984:(i + 1) * 1984].bitcast(BF16).rearrange("p (k i) -> p k i", k=31) for i in range(2)]

        def pcol(name, c=0):
            o = POFF[name] + c
            return par[:, o:o + 1]

        rot = {"f": 0, "b": 0, "bank": 0, "sm": 0}

        def tf(W):
            i = rot["f"]
            rot["f"] = (i + 1) % 6
            return tmpf[:, i, 0:W]

        def tb(W):
            i = rot["b"]
            rot["b"] = (i + 1) % 4
            return tmpb[:, i, 0:W]

        def bank():
            i = rot["bank"]
            rot["bank"] = (i + 1) % 4
            return ps[i]

        def sm(n):
            i = rot["sm"]
            if i + n > 64:
                i = 0
            rot["sm"] = i + n
            return small[:, i:i + n]

        ncv = [0]

        def convert(src, dst, r0, r1):
            cols = src.shape[1]
            a = src[r0:r1, :].rearrange("r c -> (r c)").rearrange("(n e) -> n e", e=2048)
            b = dst[r0:r1, :].rearrange("r c -> (r c)").rearrange("(n e) -> n e", e=2048)
            n = a.shape[0]
            step = 1024
            for i in range(0, n, step):
                j = min(n, i + step)
                P.dma("pool", b[i:j, :], a[i:j, :], f"cv{ncv[0]}")
                ncv[0] += 1

        convert(wkv_d, wkv_s, 0, 4 * D)
        P.dma("sp", par[:], par_d[:, :], "setup")
        P.dma("sp", flag[:], flag_d[:, :], "setup")
        P.dma("sp", RF[:, 1024:2048], bsrep_d[:, :], "setup")
        P.dma("sp", RF[:, 0:1024], wsT_d[:, :], "setup")
        P.dma("pool", biasTb[:].rearrange("p a b c -> p (a b c)"), biasT_d[:, :], "cbias")
        P.dma("pool", identb[:], ident_d[:, :], "cid")
        P.dma("pool", memTb, memT_d.rearrange("(c p) m -> p c m", p=128), "cmem")
        def convert_layer(l):
            convert(win_d[l], win_s[l], 0, D)
            convert(fin_d, fin_s, l * D, (l + 1) * D)

        NCB = 2
        CS = [sb(f"cst{i}", [128, 1024]) for i in range(NCB)]
        CO = [sb(f"cob{i}", [128, 1024], BF16) for i in range(NCB)]
        cc = [0]

        def convert_centered(src, dst, r0, r1):
            for r in range(r0, r1, 128):
                i = cc[0]
                cc[0] += 1
                st, ob = CS[i % NCB], CO[i % NCB]
                P.dma("sp", st[:], src[r:r + 128, :], f"cs{i % NCB}")
                s1 = sm(1)
                P.add("dve", lambda h, o=s1, i_=st[:]: h.tensor_reduce(o, i_, mybir.AxisListType.X, ALU.add), [st[:]], [s1])
                nm = sm(1)
                P.ts("dve", nm, s1, -1.0 / D, None, ALU.mult)
                P.act(ob[:], st[:], AF.Identity, bias=nm)
                P.dma("sp", dst[r:r + 128, :], ob[:], f"co{i % NCB}")

        LAZY = True
        convert_layer(0)
        def convert_centered_layer(l_):
            convert_centered(wo_d, wo_s, l_ * 1280, (l_ + 1) * 1280)
            convert_centered(fdn_d, fdn_s, l_ * DFF, (l_ + 1) * DFF)

        convert_centered_layer(0)
        if not LAZY:
            for l_ in range(1, 4):
                convert_layer(l_)

        P.memset("dve", onesb[:], 1.0)
        P.memset("dve", negone[:], -1.0)
        P.memset("dve", onesp[:], 0.0)
        P.memset("dve", onesp[:, 0, 0:64], 1.0)
        P.memset("dve", onesp[:, 1, 64:128], 1.0)
        P.memset("dve", Vpad[:], 0.0)
        P.memset("dve", Kdup[:], 0.0)
        P.memset("dve", gcar[:], 0.0)
        P.memset("dve", pcar[:], 0.0)
        P.memset("dve", memV[:], 0.0)
        P.copy("dve", wmT[:], wsT)
        P.memset("dve", wmT[64:128, :, 0:64], 0.0)
        for g in range(8):
            bk = bank()
            P.mm(bk[:, 0:128], onesb[:], wmT[:, g, :])
            P.stt(Cg[:, g, :], bk[:, 0:128], pcol("avln_b", g), bsrep[:, g, :], ALU.mult, ALU.add)
        P.act(esk[:], par[:, POFF["sinks"]:POFF["sinks"] + 8], AF.Exp)

        tiles = [(0, HALO, True)] + [(HALO + i * TW, TW, False) for i in range(n_main)]
        pieces = []

        def v3(kc, cols):
            return lambda s: s[:, 0:kc * cols].rearrange("p (k o) -> p k o", k=kc)

        def piece_win(src, c0, cols):
            pieces.append((src.rearrange("(k p) o -> p k o", p=128)[:, :, c0:c0 + cols], v3(8, cols)))

        for l in range(4):
            pieces.append((wkv_s[l * D:(l + 1) * D, :].rearrange("(k p) o -> p k o", p=128), v3(8, 512)))
        WIN_PIECES = [
            [(2048, 256), (0, 512), (512, 512), (1024, 512), (1536, 512)],
            [(1024, 512), (0, 512), (512, 512)],
            [(2048, 256), (0, 512), (1024, 512), (512, 512), (1536, 512)],
            [(3072, 256), (0, 512), (512, 512), (1024, 512), (2048, 512), (1536, 512), (2560, 512)],
        ]
        HALO_L3_PIECES = [(1024, 512), (2048, 512), (1536, 512), (2560, 512)]
        schedule = []
        for l in range(4):
            schedule += [(0, l), (1, l)]
        schedule += [(ti, l) for ti in range(2, len(tiles)) for l in range(4)]
        for (ti_, l) in schedule:
            if True:
                t0, W, is_halo = tiles[ti_]
                if is_halo and l == 3:
                    for (c0, cols) in HALO_L3_PIECES:
                        piece_win(win_s[l], c0, cols)
                    continue
                for (c0, cols) in WIN_PIECES[l]:
                    piece_win(win_s[l], c0, cols)
                wo_l = wo_s[l * 1280:(l + 1) * 1280, :].rearrange("(k p) o -> p k o", p=128)
                for i in range(2):
                    pieces.append((wo_l[:, :, i * 512:(i + 1) * 512], v3(10, 512)))
                fin_l = fin_s[l * D:(l + 1) * D, :].rearrange("(k p) o -> p k o", p=128)
                for i in range(11):
                    pieces.append(([fin_l[:, :, i * 256:(i + 1) * 256], fin_l[:, :, DFF + i * 256:DFF + (i + 1) * 256]],
                                   lambda s: s[:, 0:4096].rearrange("p (h k o) -> p h k o", k=8, h=2)))
                fdn_l = fdn_s[l * DFF:(l + 1) * DFF, :].rearrange("(k p) o -> p k o", p=128)
                for i in range(4):
                    pieces.append((fdn_l[:, :, i * 256:(i + 1) * 256], v3(22, 256)))

        ws = {"next": 0, "issued": 0, "held": []}

        def wget(keep=False):
            i = ws["next"]
            ws["next"] = i + 1
            if not keep:
                ws["held"] = []
            ws["held"].append(i)
            lim = min(len(pieces), min(ws["held"]) + NSLOT)
            while ws["issued"] < lim:
                j = ws["issued"]
                src, vf = pieces[j]
                if isinstance(src, list):
                    for hh, s_ in enumerate(src):
                        P.dma("sp", vf(slots[j % NSLOT])[:, hh], s_, f"w{j % NSLOT}" + ("h" if hh else ""))
                else:
                    P.dma("sp", vf(slots[j % NSLOT]), src, f"w{j % NSLOT}")
                ws["issued"] = j + 1
            assert ws["issued"] > i
            return pieces[i][1](slots[i % NSLOT])

        for l in range(4):
            w = wget()
            for hc in range(2):
                bk = bank()
                for k in range(8):
                    P.mm(bk[:, 0:256], w[:, k, hc * 128:(hc + 1) * 128], memTb[:, k, :], k == 0, k == 7)
                P.copy("act", memK[:, l, hc, :], bk[:, 0:256])
            for mc in range(2):
                bk = bank()
                for k in range(8):
                    P.mm(bk[:, 0:256], memTb[:, k, mc * 128:(mc + 1) * 128], w[:, k, 256:512], k == 0, k == 7)
                src = bk[:, 0:256].rearrange("p (hc q d) -> p hc q d", hc=2, q=2)
                for q in range(2):
                    P.copy("dve", memV[:, l, mc, q::2, q * 64:(q + 1) * 64], src[:, :, q, :])

        class LNStats:
            def __init__(self, W, delay=1, centered=False):
                self.W = W
                self.pend = []
                self.delay = delay
                self.n = 0
                self.centered = centered
                P.act(sm(1), pcol("eps"), AF.Ln)

            def push(self, c):
                W = self.W
                a = None
                if not self.centered:
                    a = tb(W)
                    P.act(a, zT[:, c, 0:W], AF.Copy)
                q = tb(W)
                P.act(q, zT[:, c, 0:W], AF.Square)
                self.pend.append((a, q))
                if len(self.pend) > self.delay:
                    self._mm()

            def _mm(self):
                W = self.W
                a, q = self.pend.pop(0)
                if a is not None:
                    P.mm(ps[4][:, 0:W], onesb[:], a, self.n == 0, self.n == 7)
                P.mm(ps[5][:, 0:W], onesb[:], q, self.n == 0, self.n == 7)
                self.n += 1

            def flush(self):
                while self.pend:
                    self._mm()
                assert self.n == 8

        def layer_norm(W, g_name, b_name, mode, out_c=None, stats=None, inplace_out=False):
            S1, S2, R = ps[4], ps[5], ps[6]
            if stats is None:
                stats = LNStats(W)
                for c in range(8):
                    stats.push(c)
            stats.flush()
            centered = stats.centered
            lv = tf(W)
            if centered:
                P.act(lv, S2[:, 0:W], AF.Ln, scale=1.0 / D, bias=pcol("eps"))
            else:
                msq = tf(W)
                P.act(msq, S1[:, 0:W], AF.Square, scale=1.0 / D)
                var = tf(W)
                P.stt(var, S2[:, 0:W], 1.0 / D, msq, ALU.mult, ALU.subtract)
                P.act(lv, var, AF.Ln, bias=pcol("eps"))
            P.act(R[:, 0:W], lv, AF.Exp, scale=-0.5)
            deferred = []

            def fin(c):
                z = zT[:, c, 0:W]
                P.tt("dve", z, R[:, 0:W], z, ALU.mult)
                g_, b_ = pcol(g_name, c), pcol(b_name, c)
                if mode == "silu":
                    P.act(out_c(c), z, AF.Silu, scale=g_, bias=b_)
                    return
                if mode == "res":
                    P.act(xb[:, c, 0:W], z, AF.Identity, scale=g_, bias=b_)
                dst = z if inplace_out else xT[:, c, 0:W]
                deferred.append(lambda: P.act(dst, z, AF.Identity, scale=g_, bias=b_))

            if centered:
                for c in range(8):
                    fin(c)
                return deferred
            for c in range(8):
                P.stt(zT[:, c, 0:W], S1[:, 0:W], -1.0 / D, zT[:, c, 0:W], ALU.mult, ALU.add)
                if c >= 1:
                    fin(c - 1)
            fin(7)
            return deferred

        def compute_mx(W):
            bk = bank()
            for k in range(8):
                P.mm(bk[:, 0:W], onesb[:], xb[:, k, 0:W], k == 0, k == 7)
            P.act(mxr[0:1, 0:W], bk[0:1, 0:W], AF.Identity, scale=ALPHA / D)

        def flush(deferred, n=None):
            k = len(deferred) if n is None else min(n, len(deferred))
            for _ in range(k):
                deferred.pop(0)()

        PXv = RB[:, 9216:9216 + 2048].rearrange("p (a b w) -> p a b w", a=2, b=2)

        def xattn_A(l, W, qxT, hc):
            for q in range(2):
                for mc in range(2):
                    bk = ps[4 + q * 2 + mc]
                    P.mm(bk[:, 0:W], memK[q * 64:(q + 1) * 64, l, hc, mc * 128:(mc + 1) * 128],
                         qxT[q * 64:(q + 1) * 64, hc, 0:W])
                    P.act(PXv[:, q, mc, 0:W], bk[:, 0:W], AF.Exp, scale=0.125)

        def xattn_B(l, W, hc):
            O, Dn = bank(), bank()
            n = 0
            for q in range(2):
                for mc in range(2):
                    P.mm(O[:, 0:W], memV[:, l, mc, 2 * hc + q, :], PXv[:, q, mc, 0:W], n == 0, n == 3)
                    n += 1
            n = 0
            for q in range(2):
                for mc in range(2):
                    P.mm(Dn[:, 0:W], onesp[:, q, :], PXv[:, q, mc, 0:W], n == 0, n == 3)
                    n += 1
            lr = tf(W)
            P.act(lr, Dn[:, 0:W], AF.Ln)
            r = tf(W)
            P.act(r, lr, AF.Exp, scale=-1.0)
            P.tt("dve", brT[:, 8 + hc, 0:W], O[:, 0:W], r, ALU.mult)

        def evac_qx(w, col0, W, qxT):
            for hc in range(2):
                bk = bank()
                for k in range(8):
                    P.mm(bk[:, 0:W], w[:, k, col0 + hc * 128:col0 + (hc + 1) * 128], xb[:, k, 0:W], k == 0, k == 7)
                P.copy("act", qxT[:, hc, 0:W], bk[:, 0:W])

        def mm_chunk(w, col, W, bk=None):
            bk = bk or bank()
            for k in range(8):
                P.mm(bk[:, 0:W], w[:, k, col:col + 128], xb[:, k, 0:W], k == 0, k == 7)
            return bk

        def qxT_view():
            return RB[:, 8192:9216].rearrange("p (c w) -> p c w", c=2)

        def mixer0(W, first_main, deferred):
            nb = W // 128
            u = RF[:, 0:4096].rearrange("p (c w) -> p c w", c=8)
            vg = RF[:, 4096:6144].rearrange("p (a f) -> p a f", a=2)
            nrm = RB[:, 0:4096].rearrange("p (b f) -> p b f", b=4)
            qxT = qxT_view()
            w = wget()
            evac_qx(w, 0, W, qxT)
            flush(deferred)
            xattn_A(0, W, qxT, 0)
            for pc in range(2):
                w = wget()
                for j in range(4):
                    c = pc * 4 + j
                    bk = mm_chunk(w, j * 128, W)
                    P.act(u[:, c, 0:W], bk[:, 0:W], AF.Gelu_apprx_tanh)
                if pc == 0:
                    xattn_B(0, W, 0)
                    xattn_A(0, W, qxT, 1)
                else:
                    xattn_B(0, W, 1)
            wa = wget()
            wb_ = wget(keep=True)
            for b in range(nb):
                vb = vg[:, b % 2, :]
                for hf, w in enumerate((wa, wb_)):
                    bk = bank()
                    for k in range(8):
                        P.mm(bk[:, :], xb[:, k, b * 128:(b + 1) * 128], w[:, k, :], k == 0, k == 7)
                    P.act(vb[:, hf * 512:(hf + 1) * 512], bk[:, :], AF.Gelu_apprx_tanh)
                st = sm(12)
                P.add("dve", lambda h, o=st[:, 0:6], i=vb[:, 0:512]: h.bn_stats(o, i), [vb[:, 0:512]], [st[:, 0:6]])
                P.add("dve", lambda h, o=st[:, 6:12], i=vb[:, 512:1024]: h.bn_stats(o, i), [vb[:, 512:1024]], [st[:, 6:12]])
                mv = sm(2)
                P.add("dve", lambda h, o=mv, i=st: h.bn_aggr(o, i), [st], [mv])
                sd = sm(1)
                P.act(sd, mv[:, 1:2], AF.Sqrt, bias=pcol("eps"))
                rs = sm(1)
                P.recip(rs, sd)
                nbias = sm(1)
                P.ts("dve", nbias, mv[:, 0:1], -1.0, rs, ALU.mult, ALU.mult)
                P.act(nrm[:, b, :], vb, AF.Identity, scale=rs, bias=nbias)
            for g in range(8):
                bk = bank()
                for b in range(nb):
                    P.mm(bk[:, b * 128:(b + 1) * 128], nrm[:, b, g * 128:(g + 1) * 128], wmT[:, g, :])
                sv = tf(W)
                P.stt(sv.rearrange("p (b i) -> p b i", i=128), bk[:, 0:W].rearrange("p (b i) -> p b i", i=128),
                      pcol("avln_g", g), Cg[:, g:g + 1, :].to_broadcast([128, nb, 128]), ALU.mult, ALU.add)
                P.tt("dve", brT[:, g, 0:W], u[:, g, 0:W], sv, ALU.mult)

        def mixer1(W, first_main, deferred):
            nb = W // 128
            Qpad = RB[:, 0:8192].rearrange("p (m q w) -> p m q w", m=8, q=2)
            qxT = qxT_view()
            P.memset(PL[0], Qpad[64:128, :, 0, :], 0.0)
            P.memset(PL[0], Qpad[0:64, :, 1, :], 0.0)
            w = wget()
            evac_qx(w, 256, W, qxT)
            flush(deferred)
            xattn_A(1, W, qxT, 0)
            for kh in range(2):
                bk = bank()
                for half in range(2):
                    for k in range(8):
                        P.mm(bk[half * 64:(half + 1) * 64, 0:W], w[:, k, kh * 64:(kh + 1) * 64], xb[:, k, 0:W],
                             k == 0, k == 7)
                P.copy("act", Kdup[:, kh, 128:128 + W], bk[:, 0:W])
            bk = bank()
            for b in range(nb):
                for k in range(8):
                    P.mm(bk[:, b * 128:(b + 1) * 128], xb[:, k, b * 128:(b + 1) * 128], w[:, k, 128:256], k == 0, k == 7)
            src = bk[:, 0:W].rearrange("p (b kh d) -> p b kh d", kh=2, d=64)
            for kh in range(2):
                for q in range(2):
                    P.copy("dve" if q else "act", Vpad[:, 1:1 + nb, kh, q, q * 64:(q + 1) * 64], src[:, :, kh, :])
            for pc in range(2):
                w = wget()
                for j in range(4):
                    m = pc * 4 + j
                    bk = mm_chunk(w, j * 128, W)
                    P.act(Qpad[0:64, m, 0, 0:W], bk[0:64, 0:W], AF.Identity, scale=0.125)
                    P.ts("dve", Qpad[64:128, m, 1, 0:W], bk[64:128, 0:W], 0.125, None, ALU.mult)
                if pc == 0:
                    xattn_B(1, W, 0)
                    xattn_A(1, W, qxT, 1)
                else:
                    xattn_B(1, W, 1)
            PTbuf = RB[:, 9216:11264]
            for qb in range(nb):
                for kh in range(2):
                    pts = []
                    for bi in range(2):
                        kcols = slice((qb + bi) * 128, (qb + bi + 1) * 128)
                        pt = PTbuf[:, bi * 1024:(bi + 1) * 1024]
                        for hf in range(2):
                            bk = ps[bi * 2 + hf]
                            h0 = kh * 8 + hf * 4
                            P.mm(bk[:, :], identb[:], biasTb[:, bi, h0:h0 + 4, :].rearrange("p h q -> p (h q)"), True, False)
                            for mm_ in range(2):
                                m = hf * 2 + mm_
                                P.mm(bk[:, mm_ * 256:(mm_ + 1) * 256].rearrange("p (q w) -> p q w", q=2),
                                     Kdup[:, kh, kcols], Qpad[:, kh * 4 + m, :, qb * 128:(qb + 1) * 128],
                                     False, mm_ == 1)
                            P.act(pt[:, hf * 512:(hf + 1) * 512], bk[:, :], AF.Exp)
                        if first_main and qb == 0 and bi == 0:
                            P.ts("dve", pt, pt, flag[:, 0:1], None, ALU.mult)
                        pts.append(pt)
                    O, Dn = ps[4], ps[5]
                    n = 0
                    for bi in range(2):
                        pv = pts[bi].rearrange("p (m q w) -> p m q w", m=4, q=2)
                        for q in range(2):
                            P.mm(O[:, :].rearrange("p (m w) -> p m w", m=4), Vpad[:, qb + bi, kh, q, :], pv[:, :, q, :],
                                 n == 0, n == 3)
                            n += 1
                    n = 0
                    for bi in range(2):
                        pv = pts[bi].rearrange("p (m q w) -> p m q w", m=4, q=2)
                        for q in range(2):
                            P.mm(Dn[:, :].rearrange("p (m w) -> p m w", m=4), onesp[:, q, :], pv[:, :, q, :],
                                 n == 0, n == 3)
                            n += 1
                    dsum = tf(512)
                    P.tt("dve", dsum.rearrange("p (m w) -> p m w", m=4), Dn[:, :].rearrange("p (m w) -> p m w", m=4),
                         esk[:, kh * 4:(kh + 1) * 4].unsqueeze(2).to_broadcast([128, 4, 128]), ALU.add)
                    lr = tf(512)
                    P.act(lr, dsum, AF.Ln)
                    r = tf(512)
                    P.act(r, lr, AF.Exp, scale=-1.0)
                    P.tt("dve", brT[:, kh * 4:(kh + 1) * 4, qb * 128:(qb + 1) * 128],
                         O[:, :].rearrange("p (m w) -> p m w", m=4), r.rearrange("p (m w) -> p m w", m=4), ALU.mult)
            P.copy(PL[0], Kdup[:, :, 0:128], Kdup[:, :, W:W + 128])
            P.copy(PL[0], Vpad[:, 0], Vpad[:, nb])

        def mixer2(W, first_main, deferred):
            glu = RB[:, 0:8 * 544].rearrange("p (c w) -> p c w", c=8)
            qxT = qxT_view()
            P.copy(PL[0], glu[:, :, 0:32], gcar[:])
            if first_main:
                P.ts("dve", glu[:, :, 0:32], glu[:, :, 0:32], flag[:, 0:1], None, ALU.mult)
            w = wget()
            evac_qx(w, 0, W, qxT)
            flush(deferred)
            xattn_A(2, W, qxT, 0)
            for pc in range(2):
                wa = wget()
                wg = wget(keep=True)
                for j in range(4):
                    c = pc * 4 + j
                    ba = mm_chunk(wa, j * 128, W)
                    bg_ = mm_chunk(wg, j * 128, W)
                    sg = tf(W)
                    P.act(sg, bg_[:, 0:W], AF.Sigmoid)
                    P.tt("dve", glu[:, c, 32:32 + W], ba[:, 0:W], sg, ALU.mult)
                if pc == 0:
                    xattn_B(2, W, 0)
                    xattn_A(2, W, qxT, 1)
                else:
                    xattn_B(2, W, 1)
            st2 = LNStats(W)
            for c in range(8):
                dgc = DG[c % 2]
                o = POFF["cconv_w"] + c * 31
                P.tt(PL[0], dgc, identb[:].unsqueeze(1).to_broadcast([128, 31, 128]),
                     par[:, o:o + 31].unsqueeze(2).to_broadcast([128, 31, 128]), ALU.mult)
                bk = bank()
                for k in range(31):
                    P.mm(bk[:, 0:W], dgc[:, k, :], glu[:, c, 2 + k:2 + k + W], k == 0, k == 30)
                P.act(zT[:, c, 0:W], bk[:, 0:W], AF.Identity, bias=pcol("cconv_b", c))
                st2.push(c)
            P.copy(PL[0], gcar[:], glu[:, :, W:W + 32])
            layer_norm(W, "cln_g", "cln_b", "silu", out_c=lambda c: brT[:, c, 0:W], stats=st2)

        def mixer3(W, first_main, deferred, halo=False):
            bgs = RF[:, 0:4096].rearrange("p (c w) -> p c w", c=8)
            pb = RB[:, 0:8 * 514].rearrange("p (c w) -> p c w", c=8)
            qxT = qxT_view()
            P.copy(PL[0], pb[:, :, 0:2], pcar[:])
            if first_main:
                P.ts("dve", pb[:, :, 0:2], pb[:, :, 0:2], flag[:, 0:1], None, ALU.mult)
            if not halo:
                for c in range(8):
                    o = POFF["dconv_w"] + c * 3
                    P.tt(PL[0], diag3[:, c], identb[:].unsqueeze(1).to_broadcast([128, 3, 128]),
                         par[:, o:o + 3].unsqueeze(2).to_broadcast([128, 3, 128]), ALU.mult)
                w = wget()
                evac_qx(w, 0, W, qxT)
                flush(deferred)
                xattn_A(3, W, qxT, 0)
                for pc in range(2):
                    w = wget()
                    for j in range(4):
                        c = pc * 4 + j
                        bk = mm_chunk(w, j * 128, W)
                        P.copy("act", bgs[:, c, 0:W], bk[:, 0:W])
                    if pc == 0:
                        xattn_B(3, W, 0)
                        xattn_A(3, W, qxT, 1)
                    else:
                        xattn_B(3, W, 1)
            else:
                flush(deferred)
            for pc in range(2):
                wc = wget()
                wh = wget(keep=True)
                for j in range(4):
                    c = pc * 4 + j
                    bc = mm_chunk(wc, j * 128, W)
                    bh = mm_chunk(wh, j * 128, W)
                    cs = tf(W)
                    P.copy("act", cs, bc[:, 0:W])
                    P.tt("dve", pb[:, c, 2:2 + W], bh[:, 0:W], cs, ALU.mult)
            P.copy(PL[0], pcar[:], pb[:, :, W:W + 2])
            if halo:
                return
            for c in range(8):
                bk = bank()
                for k in range(3):
                    P.mm(bk[:, 0:W], diag3[:, c, k, :], pb[:, c, k:k + W], k == 0, k == 2)
                P.tt("dve", brT[:, c, 0:W], bk[:, 0:W], bgs[:, c, 0:W], ALU.mult)

        mixers = [mixer0, mixer1, mixer2, mixer3]

        out_tokens = []
        xT_dv = xT_d.rearrange("(c p) t -> p c t", p=128)
        outT_dv = outT_d.rearrange("(c p) t -> p c t", p=128)

        def load_x_dma(ti):
            t0, W, _ = tiles[ti]
            P.dma("sp", xT[:, :, 0:W], xT_dv[:, :, t0:t0 + W], "xin_h" if ti == 0 else "xin")

        def load_x_cast(ti):
            t0, W, _ = tiles[ti]
            for c in range(8):
                P.copy(("dve" if ti <= 1 else "pool") if c % 2 else "act", xb[:, c, 0:W], xT[:, c, 0:W])

        defer_t = {}
        PL = ["pool"]

        def run_layer(ti, l):
            t0, W, is_halo = tiles[ti]
            deferred = defer_t.get(ti, [])
            if is_halo and l == 3:
                mixer3(W, False, deferred, halo=True)
                return
            mixers[l](W, ti == 1, deferred)
            assert not deferred
            st1 = LNStats(W, centered=True)
            compute_mx(W)
            for pc in range(2):
                w = wget()
                for j in range(4):
                    oc = pc * 4 + j
                    bk = bank()
                    for k in range(10):
                        P.mm(bk[:, 0:W], w[:, k, j * 128:(j + 1) * 128], brT[:, k, 0:W], k == 0, False)
                    P.mm(bk[:, 0:W], negone[0:1, :], mxr[0:1, 0:W], False, True)
                    P.stt(zT[:, oc, 0:W], xT[:, oc, 0:W], ALPHA, bk[:, 0:W], ALU.mult, ALU.add)
                    st1.push(oc)
            deferred = layer_norm(W, f"ln1_g{l}", f"ln1_b{l}", "res", stats=st1)
            prefetch = (l == 3) and (ti + 1 < len(tiles))
            XP = RF[:, 0:4096].rearrange("p (c w) -> p c w", c=8)
            if prefetch:
                tn = tiles[ti + 1][0]
                P.dma("sp", XP, xT_dv[:, :, tn:tn + TW], "xin")
            gT = RB[:, 0:22 * TW].rearrange("p (c w) -> p c w", c=22)
            for i in range(11):
                w = wget()
                for j in range(2):
                    ch = 2 * i + j
                    b1 = bank()
                    for k in range(8):
                        P.mm(b1[:, 0:W], w[:, 0, k, j * 128:(j + 1) * 128], xb[:, k, 0:W], k == 0, k == 7)
                    b2 = bank()
                    for k in range(8):
                        P.mm(b2[:, 0:W], w[:, 1, k, j * 128:(j + 1) * 128], xb[:, k, 0:W], k == 0, k == 7)
                    s1 = tf(W)
                    P.act(s1, b1[:, 0:W], AF.Silu)
                    P.tt("dve", gT[:, ch, 0:W], b2[:, 0:W], s1, ALU.mult)
                    flush(deferred, 1)
                if i == 1:
                    compute_mx(W)
            assert not deferred
            if prefetch:
                for c in range(8):
                    P.copy("act", xb[:, c, :], XP[:, c, :])
            st3 = LNStats(W, centered=True)
            for pc in range(4):
                w = wget()
                for j in range(2):
                    oc = pc * 2 + j
                    bk = bank()
                    for k in range(22):
                        P.mm(bk[:, 0:W], w[:, k, j * 128:(j + 1) * 128], gT[:, k, 0:W], k == 0, False)
                    P.mm(bk[:, 0:W], negone[0:1, :], mxr[0:1, 0:W], False, True)
                    P.stt(zT[:, oc, 0:W], xT[:, oc, 0:W], ALPHA, bk[:, 0:W], ALU.mult, ALU.add)
                    st3.push(oc)
            if l < 3:
                defer_t[ti] = layer_norm(W, f"ln2_g{l}", f"ln2_b{l}", "res", stats=st3)
                if ti <= 1:
                    flush(defer_t[ti])
            else:
                if prefetch:
                    P.dma("sp", xT[:, :, :], XP, "xcp")
                deferred = layer_norm(W, f"ln2_g{l}", f"ln2_b{l}", "f32only", stats=st3, inplace_out=True)

                def store(t0=t0, W=W):
                    out_tokens.append(P.dma("sp", outT_dv[:, :, t0 - HALO:t0 - HALO + W], zT[:, :, 0:W], "out"))

                deferred.append(store)
                defer_t[ti] = []
                if prefetch:
                    defer_t[ti + 1] = deferred
                else:
                    flush(deferred)

        xT_m, xb_m = xT, xb
        xT, xb = xT_h, xb_h
        load_x_dma(0)
        load_x_cast(0)
        xT, xb = xT_m, xb_m
        load_x_dma(1)
        load_x_cast(1)
        for (ti, l) in schedule:
            if ti == 0:
                xT, xb = xT_h, xb_h
                if l + 1 < 4 and LAZY:
                    convert_layer(l + 1)
                    convert_centered_layer(l + 1)
            else:
                xT, xb = xT_m, xb_m
            if ti == 1 and l == 3 and len(tiles) == 2:
                pass
            PL[0] = "dve" if (ti <= 1 and LAZY) else "pool"
            run_layer(ti, l)
        P.add("sp", None, extra_deps=out_tokens[-1:])
        assert ws["next"] == len(pieces), (ws["next"], len(pieces))

        engsem = {e: es.enter_context(nc.semaphore(f"sem_{e}")) for e in Prog.ENGS}
        dmasem = {k: es.enter_context(nc.semaphore(f"dsem_{k}")) for k in P.dma_cnt}
        with nc.Block() as block:
            P.emit(block, engsem, dmasem)
    return nc


def _host_prep(inp, n_main=NMAIN, cores=None):
    x = np.asarray(inp["x"], np.float32)
    mem = np.asarray(inp["mem"], np.float32)
    B, S, _ = x.shape
    per = n_main * TW
    par = np.zeros((128, NPAR), np.float32)

    def put(name, arr):
        arr = np.asarray(arr, np.float32)
        par[:, POFF[name]:POFF[name] + arr.shape[1]] = arr

    for l in range(4):
        put(f"ln1_g{l}", _fm(inp["ln1_g"][l]))
        put(f"ln1_b{l}", _fm(inp["ln1_b"][l]))
        put(f"ln2_g{l}", _fm(inp["ln2_g"][l]))
        put(f"ln2_b{l}", _fm(inp["ln2_b"][l]))
    put("avln_g", _fm(inp["a_v_ln_g"][0]))
    put("avln_b", _fm(inp["a_v_ln_b"][0]))
    put("cln_g", _fm(inp["c_ln_g"][0]))
    put("cln_b", _fm(inp["c_ln_b"][0]))
    put("cconv_b", _fm(inp["c_conv_b"][0]))
    sinks = np.asarray(inp["b_sinks"], np.float32)[0]
    srep = np.zeros((128, 8), np.float32)
    for kh in range(2):
        for m in range(4):
            srep[0:64, kh * 4 + m] = sinks[kh * 8 + 2 * m]
            srep[64:128, kh * 4 + m] = sinks[kh * 8 + 2 * m + 1]
    put("sinks", srep)
    cw = np.asarray(inp["c_conv_w"], np.float32)[0]
    put("cconv_w", cw.T.reshape(8, 128, 31).transpose(1, 0, 2).reshape(128, 8 * 31))
    dw = np.asarray(inp["d_conv_w"], np.float32)[0]
    put("dconv_w", dw.T.reshape(8, 128, 3).transpose(1, 0, 2).reshape(128, 24))
    par[:, POFF["eps"]] = EPS

    bs = np.asarray(inp["a_b_s"], np.float32)[0]
    bs_rep = np.ascontiguousarray(np.broadcast_to(bs.reshape(1, 8 * 128), (128, 8 * 128)))
    ws = np.asarray(inp["a_w_s"], np.float32)[0]
    wsT = np.ascontiguousarray(ws.transpose(2, 0, 1).reshape(128, 8 * 128))
    rb = np.asarray(inp["rel_bias"], np.float32)
    q = np.arange(128)[None, :]
    biasT = np.zeros((128, 2, 16, 128), np.float32)
    for bi in range(2):
        kpos = (np.arange(128) + (bi - 1) * 128)[:, None]
        bucket = _t5_bucket(kpos - q)
        kc = np.floor_divide(kpos, 64)
        qc = q // 64
        valid = (kc >= qc - 2) & (kc <= qc)
        g = rb[bucket]
        g = np.where(valid[:, :, None], g, np.float32(-1e30))
        biasT[:, bi] = g.transpose(0, 2, 1)
    biasT = np.ascontiguousarray(biasT.reshape(128, 2 * 16 * 128))
    shared = {
        "params": par, "bs_rep": bs_rep, "wsT": wsT, "biasT": biasT,
        "ident": np.eye(128, dtype=np.float32),
        "a_w_in": np.ascontiguousarray(inp["a_w_in"][0], np.float32),
        "b_w_in": np.ascontiguousarray(inp["b_w_in"][0], np.float32),
        "c_w_in": np.ascontiguousarray(inp["c_w_in"][0], np.float32),
        "d_w_in": np.ascontiguousarray(inp["d_w_in"][0], np.float32),
        "w_mem_kv": np.ascontiguousarray(np.asarray(inp["w_mem_kv"], np.float32).reshape(4 * D, 512)),
        "w_o": np.ascontiguousarray(np.asarray(inp["w_o"], np.float32).reshape(4 * 1280, D)),
        "ffn_w_in": np.ascontiguousarray(np.asarray(inp["ffn_w_in"], np.float32).reshape(4 * D, 2 * DFF)),
        "ffn_w_down": np.ascontiguousarray(np.asarray(inp["ffn_w_down"], np.float32).reshape(4 * DFF, D)),
    }
    if cores is None:
        cores = [(b, s0) for b in range(B) for s0 in range(0, S, per)]
    in_maps = []
    for (b, s0) in cores:
        xe = np.zeros((HALO + per, D), np.float32)
        if s0 > 0:
            xe[:] = x[b, s0 - HALO:s0 + per]
            fl = 1.0
        else:
            xe[HALO:] = x[b, 0:per]
            fl = 0.0
        m = dict(shared)
        m["xT"] = np.ascontiguousarray(xe.T)
        m["memT"] = np.ascontiguousarray(mem[b].T)
        m["flag"] = np.full((128, 1), fl, np.float32)
        in_maps.append(m)
    return in_maps, cores


def kernel(**inputs):
    in_maps, cores = _host_prep(inputs)
    nc = build(NMAIN)
    res = run_bass_kernel_spmd(nc, in_maps, core_ids=list(range(NCORES)))
    x = inputs["x"]
    B, S, _ = x.shape
    out = np.empty((B, S, D), np.float32)
    per = NMAIN * TW
    for (b, s0), r in zip(cores, res.results):
        out[b, s0:s0 + per] = np.asarray(r["outT"]).T
    return out
```

```python
import numpy as np
from contextlib import ExitStack
import concourse.bass as bass
import concourse.mybir as mybir
from concourse.bass_utils import run_bass_kernel_spmd

F32 = mybir.dt.float32
BF16 = mybir.dt.bfloat16
AF = mybir.ActivationFunctionType
ALU = mybir.AluOpType

D = 1024
DFF = 2816
HALO = 256
TW = 512
NMAIN = 8
NCORES = 8
ALPHA = float(8.0 ** 0.25)
EPS = 1e-5
NSLOT = 3
SLOT_ELEMS = 22 * 256

W_IN_COLS = [2304, 1536, 2304, 3328]
W_IN_NAMES = ["a_w_in", "b_w_in", "c_w_in", "d_w_in"]


def _param_layout():
    off = {}
    cur = 0

    def add(name, n):
        nonlocal cur
        off[name] = cur
        cur += n

    for l in range(4):
        for which in ("ln1", "ln2"):
            add(f"{which}_g{l}", 8)
            add(f"{which}_b{l}", 8)
    add("avln_g", 8)
    add("avln_b", 8)
    add("cln_g", 8)
    add("cln_b", 8)
    add("cconv_b", 8)
    add("sinks", 8)
    add("cconv_w", 8 * 31)
    add("dconv_w", 8 * 3)
    add("eps", 1)
    return off, cur


POFF, NPAR = _param_layout()


def _fm(v):
    return np.ascontiguousarray(np.asarray(v, np.float32).reshape(8, 128).T)


def _t5_bucket(rel):
    nb = 16
    ret = (rel > 0).astype(np.int32) * nb
    n = np.abs(rel)
    max_exact = nb // 2
    large = max_exact + (np.log(np.maximum(n, 1) / max_exact)
                         / np.log(128 / max_exact) * (nb - max_exact)).astype(np.int32)
    large = np.minimum(large, nb - 1)
    return (ret + np.where(n < max_exact, n, large)).astype(np.int32)


class Prog:
    ENGS = ["pe", "act", "dve", "pool", "sp"]

    def __init__(self, nc):
        self.nc = nc
        self.ops = {e: [] for e in self.ENGS}
        self.track = {}
        self.dma_cnt = {}
        self.signal = set()
        self.seen = {e: {} for e in self.ENGS}
        self.waitall_keys = set()

    @staticmethod
    def box(ap):
        a = ap.ap
        off = ap.offset
        name = ap.tensor.name
        if "DRAM" in str(ap.space).upper():
            ext = sum((c - 1) * abs(st) for st, c in a) + 1
            return (name, 0, 1, off, off + ext, True)
        row = a[0][0]
        npart = a[0][1]
        sz = mybir.dt.size(ap.dtype)
        p0 = off // row if row > 0 else 0
        f0 = (off - p0 * row) * sz
        ext = (sum((c - 1) * abs(st) for st, c in a[1:]) + 1) * sz
        return (name, p0, p0 + npart, f0, f0 + ext, False)

    @staticmethod
    def _ov(a, b):
        return a[1] < b[2] and b[1] < a[2] and a[3] < b[4] and b[3] < a[4]

    @staticmethod
    def _cov(a, b):
        return a[1] <= b[1] and a[2] >= b[2] and a[3] <= b[3] and a[4] >= b[4]

    def add(self, eng, fn, reads=(), writes=(), dma=None, extra_deps=()):
        idx = len(self.ops[eng])
        deps = set(extra_deps)
        rb = [self.box(ap) for ap in reads]
        wb = [self.box(ap) for ap in writes]
        for b in rb:
            ents = self.track.get(b[0])
            if ents:
                for k, e in ents.items():
                    if e["w"] is not None and self._ov(k, b):
                        deps.add(e["w"])
        for b in wb:
            ents = self.track.get(b[0])
            if ents:
                for k in list(ents.keys()):
                    e = ents[k]
                    if self._ov(k, b):
                        if e["w"] is not None:
                            deps.add(e["w"])
                        deps.update(e["r"].values())
                        deps.update(e["rd"])
                        if k != b and self._cov(b, k):
                            del ents[k]
        if dma is not None:
            cnt = self.dma_cnt.get(dma, 0) + 16
            self.dma_cnt[dma] = cnt
            token = ("dma", dma, cnt)
        else:
            token = (eng, idx)
        for b in rb:
            if b[5]:
                continue
            ents = self.track.setdefault(b[0], {})
            e = ents.get(b)
            if e is None:
                e = {"w": None, "r": {}, "rd": []}
                ents[b] = e
            if dma is not None:
                e["rd"].append(token)
            else:
                e["r"][eng] = token
        for b in wb:
            self.track.setdefault(b[0], {})[b] = {"w": token, "r": {}, "rd": []}
        best = {}
        seen = self.seen[eng]
        for d in deps:
            if d[0] == "dma":
                k = ("dma", d[1])
                if d[1] in self.waitall_keys:
                    if seen.get(k, 0) < (1 << 60):
                        seen[k] = 1 << 60
                        best[k] = None
                    continue
                if seen.get(k, 0) >= d[2]:
                    continue
                if k not in best or (best[k] is not None and best[k] < d[2]):
                    best[k] = d[2]
            else:
                if d[0] == "pe" and eng == "pe":
                    continue
                if seen.get(d[0], -1) >= d[1]:
                    continue
                if d[0] not in best or best[d[0]] < d[1]:
                    best[d[0]] = d[1]
        waits = []
        for k, v in best.items():
            if isinstance(k, tuple):
                if v is not None:
                    seen[k] = max(seen.get(k, 0), v)
                waits.append(("dma", k[1], v))
            else:
                seen[k] = v
                waits.append((k, v))
                self.signal.add((k, v))
        self.ops[eng].append({"fn": fn, "waits": waits, "dma": dma})
        return token

    def mm(self, out, lhsT, rhs, start=True, stop=True):
        return self.add("pe", lambda h: h.matmul(out, lhsT=lhsT, rhs=rhs, start=start, stop=stop),
                        [lhsT, rhs], [out])

    def act(self, out, in_, func, scale=None, bias=None):
        reads = [in_]
        kw = {}
        if scale is not None:
            kw["scale"] = scale
            if not isinstance(scale, (int, float)):
                reads.append(scale)
        if bias is not None:
            kw["bias"] = bias
            if not isinstance(bias, (int, float)):
                reads.append(bias)
        return self.add("act", lambda h: h.activation(out, in_, func, **kw), reads, [out])

    def tt(self, eng, out, in0, in1, op):
        return self.add(eng, lambda h: h.tensor_tensor(out, in0, in1, op), [in0, in1], [out])

    def ts(self, eng, out, in0, s1, s2, op0, op1=None):
        reads = [in0] + [s for s in (s1, s2) if s is not None and not isinstance(s, (int, float))]
        if op1 is None:
            return self.add(eng, lambda h: h.tensor_scalar(out, in0, s1, None, op0), reads, [out])
        return self.add(eng, lambda h: h.tensor_scalar(out, in0, s1, s2, op0, op1), reads, [out])

    def stt(self, out, in0, scalar, in1, op0, op1):
        reads = [in0, in1] + ([] if isinstance(scalar, (int, float)) else [scalar])
        return self.add("dve", lambda h: h.scalar_tensor_tensor(out, in0, scalar, in1, op0, op1), reads, [out])

    def copy(self, eng, out, in_):
        if eng == "act":
            return self.act(out, in_, AF.Copy)
        return self.add(eng, lambda h: h.tensor_copy(out, in_), [in_], [out])

    def recip(self, out, in_):
        return self.add("dve", lambda h: h.reciprocal(out, in_), [in_], [out])

    def memset(self, eng, ap, val):
        return self.add(eng, lambda h: h.memset(ap, val), [], [ap])

    def dma(self, eng, out, in_, key):
        return self.add(eng, lambda h: h.dma_start(out=out, in_=in_), [in_], [out], dma=key)

    def emit(self, block, engsem, dmasem):
        sigcount = {}
        for e in self.ENGS:
            c = 0
            for i in range(len(self.ops[e])):
                if (e, i) in self.signal:
                    c += 1
                    sigcount[(e, i)] = c
        handles = {"pe": block.tensor, "act": block.scalar, "dve": block.vector,
                   "pool": block.gpsimd, "sp": block.sync}

        def make(e):
            ops = self.ops[e]

            def body(h):
                for i, op in enumerate(ops):
                    for wt in op["waits"]:
                        if wt[0] == "dma":
                            cnt = wt[2] if wt[2] is not None else self.dma_cnt[wt[1]]
                            h.wait_ge(dmasem[wt[1]], cnt)
                        else:
                            h.wait_ge(engsem[wt[0]], sigcount[wt])
                    if op["fn"] is None:
                        continue
                    ins = op["fn"](h)
                    if op["dma"] is not None:
                        ins.then_inc(dmasem[op["dma"]], 16)
                    elif (e, i) in self.signal:
                        ins.then_inc(engsem[e], 1)
            return body

        for e in self.ENGS:
            handles[e](make(e))


def build(n_main=NMAIN):
    nc = bass.Bass("TRN2", target_bir_lowering=False)
    TTOK = HALO + n_main * TW
    P = Prog(nc)
    P.waitall_keys.add("setup")

    def din(name, shape, dt=F32):
        return nc.dram_tensor(name, list(shape), dt, kind="ExternalInput").ap()

    xT_d = din("xT", [D, TTOK])
    memT_d = din("memT", [D, 256])
    flag_d = din("flag", [128, 1])
    par_d = din("params", [128, NPAR])
    bsrep_d = din("bs_rep", [128, 8 * 128])
    wsT_d = din("wsT", [128, 8 * 128])
    biasT_d = din("biasT", [128, 2 * 16 * 128])
    ident_d = din("ident", [128, 128])
    win_d = [din(W_IN_NAMES[l], [D, W_IN_COLS[l]]) for l in range(4)]
    wkv_d = din("w_mem_kv", [4 * D, 512])
    wo_d = din("w_o", [4 * 1280, D])
    fin_d = din("ffn_w_in", [4 * D, 2 * DFF])
    fdn_d = din("ffn_w_down", [4 * DFF, D])
    outT_d = nc.dram_tensor("outT", [D, n_main * TW], F32, kind="ExternalOutput").ap()

    def dscr(name, shape):
        return nc.dram_tensor(name, list(shape), BF16, kind="Internal").ap()

    win_s = [dscr(f"s_win{l}", [D, W_IN_COLS[l]]) for l in range(4)]
    wkv_s = dscr("s_wkv", [4 * D, 512])
    wo_s = dscr("s_wo", [4 * 1280, D])
    fin_s = dscr("s_fin", [4 * D, 2 * DFF])
    fdn_s = dscr("s_fdn", [4 * DFF, D])

    es = ExitStack()
    with es:
        def sb(name, shape, dt=F32):
            return es.enter_context(nc.sbuf_tensor(name, list(shape), dt))

        xT = sb("xT_sb", [128, 8, TW])
        xb = sb("xb_sb", [128, 8, TW], BF16)
        xT_h = sb("xT_h", [128, 8, HALO])
        xb_h = sb("xb_h", [128, 8, HALO], BF16)
        brT = sb("brT", [128, 10, TW], BF16)
        zT = sb("zT", [128, 8, TW])
        tmpf = sb("tmpf", [128, 6, TW])
        tmpb = sb("tmpb", [128, 4, TW], BF16)
        slots = [sb(f"wslot{i}", [128, SLOT_ELEMS], BF16) for i in range(NSLOT)]
        RF = sb("regf", [128, 6144])
        RB = sb("regb", [128, 11264], BF16)
        Kdup = sb("Kdup", [128, 2, 128 + TW], BF16)
        Vpad = sb("Vpad", [128, 5, 2, 2, 128], BF16)
        gcar = sb("gcar", [128, 8, 32], BF16)
        pcar = sb("pcar", [128, 8, 2], BF16)
        memK = sb("memK", [128, 4, 2, 256], BF16)
        memV = sb("memV", [128, 4, 2, 4, 128], BF16)
        biasTb = sb("biasTb", [128, 2, 16, 128], BF16)
        wmT = sb("wmT", [128, 8, 128], BF16)
        Cg = sb("Cg", [128, 8, 128])
        par = sb("par", [128, NPAR])
        flag = sb("flag_sb", [128, 1])
        esk = sb("esk", [128, 8])
        identb = sb("identb", [128, 128], BF16)
        onesb = sb("onesb", [128, 128], BF16)
        onesp = sb("onesp", [128, 2, 128], BF16)
        small = sb("small", [128, 64])
        negone = sb("negone", [1, 128], BF16)
        mxr = sb("mxr", [1, TW], BF16)
        ps = [es.enter_context(nc.psum_tensor(f"ps{i}", [128, 512], F32)) for i in range(8)]
        wsT = RF[:, 0:1024].rearrange("p (a b) -> p a b", a=8)
        bsrep = RF[:, 1024:2048].rearrange("p (a b) -> p a b", a=8)
        memTb = RB[:, 0:2048].rearrange("p (a b) -> p a b", a=8)
        diag3 = RB[:, 4608:7680].rearrange("p (c k i) -> p c k i", c=8, k=3)
        DG = [RF[:, i * 1984:(i + 1) * 1984].bitcast(BF16).rearrange("p (k i) -> p k i", k=31) for i in range(2)]

        def pcol(name, c=0):
            o = POFF[name] + c
            return par[:, o:o + 1]

        rot = {"f": 0, "b": 0, "bank": 0, "sm": 0}

        def tf(W):
            i = rot["f"]
            rot["f"] = (i + 1) % 6
            return tmpf[:, i, 0:W]

        def tb(W):
            i = rot["b"]
            rot["b"] = (i + 1) % 4
            return tmpb[:, i, 0:W]

        def bank():
            i = rot["bank"]
            rot["bank"] = (i + 1) % 4
            return ps[i]

        def sm(n):
            i = rot["sm"]
            if i + n > 64:
                i = 0
            rot["sm"] = i + n
            return small[:, i:i + n]

        ncv = [0]

        def convert(src, dst, r0, r1):
            cols = src.shape[1]
            a = src[r0:r1, :].rearrange("r c -> (r c)").rearrange("(n e) -> n e", e=2048)
            b = dst[r0:r1, :].rearrange("r c -> (r c)").rearrange("(n e) -> n e", e=2048)
            n = a.shape[0]
            step = 1024
            for i in range(0, n, step):
                j = min(n, i + step)
                P.dma("pool", b[i:j, :], a[i:j, :], f"cv{ncv[0]}")
                ncv[0] += 1

        convert(wkv_d, wkv_s, 0, 4 * D)
        P.dma("sp", par[:], par_d[:, :], "setup")
        P.dma("sp", flag[:], flag_d[:, :], "setup")
        P.dma("sp", RF[:, 1024:2048], bsrep_d[:, :], "setup")
        P.dma("sp", RF[:, 0:1024], wsT_d[:, :], "setup")
        P.dma("pool", biasTb[:].rearrange("p a b c -> p (a b c)"), biasT_d[:, :], "cbias")
        P.dma("pool", identb[:], ident_d[:, :], "cid")
        P.dma("pool", memTb, memT_d.rearrange("(c p) m -> p c m", p=128), "cmem")
        def convert_layer(l):
            convert(win_d[l], win_s[l], 0, D)
            convert(fin_d, fin_s, l * D, (l + 1) * D)

        NCB = 2
        CS = [sb(f"cst{i}", [128, 1024]) for i in range(NCB)]
        CO = [sb(f"cob{i}", [128, 1024], BF16) for i in range(NCB)]
        cc = [0]

        cen_jobs = []
        cen_prev = [None]

        def convert_centered(src, dst, r0, r1):
            for r in range(r0, r1, 128):
                cen_jobs.append((src[r:r + 128, :], dst[r:r + 128, :]))

        def cen_tick():
            nxt = None
            if cen_jobs:
                i = cc[0]
                cc[0] += 1
                sa, da = cen_jobs.pop(0)
                P.dma("sp", CS[i % NCB][:], sa, f"cs{i % NCB}")
                nxt = (i, da)
            if cen_prev[0] is not None:
                i, da = cen_prev[0]
                st, ob = CS[i % NCB], CO[i % NCB]
                s1 = sm(1)
                P.add("dve", lambda h, o=s1, i_=st[:]: h.tensor_reduce(o, i_, mybir.AxisListType.X, ALU.add), [st[:]], [s1])
                nm = sm(1)
                P.ts("dve", nm, s1, -1.0 / D, None, ALU.mult)
                P.act(ob[:], st[:], AF.Identity, bias=nm)
                P.dma("sp", da, ob[:], f"co{i % NCB}")
            cen_prev[0] = nxt

        def cen_flush():
            while cen_jobs or cen_prev[0] is not None:
                cen_tick()

        LAZY = True
        convert_layer(0)

        def convert_centered_layer(l_):
            convert_centered(wo_d, wo_s, l_ * 1280, (l_ + 1) * 1280)
            convert_centered(fdn_d, fdn_s, l_ * DFF, (l_ + 1) * DFF)

        convert_centered_layer(0)
        cen_flush()
        if not LAZY:
            for l_ in range(1, 4):
                convert_layer(l_)

        P.memset("dve", onesb[:], 1.0)
        P.memset("dve", negone[:], -1.0)
        P.memset("dve", onesp[:], 0.0)
        P.memset("dve", onesp[:, 0, 0:64], 1.0)
        P.memset("dve", onesp[:, 1, 64:128], 1.0)
        P.memset("dve", Vpad[:], 0.0)
        P.memset("dve", Kdup[:], 0.0)
        P.memset("dve", gcar[:], 0.0)
        P.memset("dve", pcar[:], 0.0)
        P.memset("dve", memV[:], 0.0)
        P.copy("dve", wmT[:], wsT)
        P.memset("dve", wmT[64:128, :, 0:64], 0.0)
        for g in range(8):
            bk = bank()
            P.mm(bk[:, 0:128], onesb[:], wmT[:, g, :])
            P.stt(Cg[:, g, :], bk[:, 0:128], pcol("avln_b", g), bsrep[:, g, :], ALU.mult, ALU.add)
        P.act(esk[:], par[:, POFF["sinks"]:POFF["sinks"] + 8], AF.Exp)

        tiles = [(0, HALO, True)] + [(HALO + i * TW, TW, False) for i in range(n_main)]
        pieces = []

        def v3(kc, cols):
            return lambda s: s[:, 0:kc * cols].rearrange("p (k o) -> p k o", k=kc)

        def piece_win(src, c0, cols):
            pieces.append((src.rearrange("(k p) o -> p k o", p=128)[:, :, c0:c0 + cols], v3(8, cols)))

        for l in range(4):
            pieces.append((wkv_s[l * D:(l + 1) * D, :].rearrange("(k p) o -> p k o", p=128), v3(8, 512)))
        WIN_PIECES = [
            [(2048, 256), (0, 512), (512, 512), (1024, 512), (1536, 512)],
            [(1024, 512), (0, 512), (512, 512)],
            [(2048, 256), (0, 512), (1024, 512), (512, 512), (1536, 512)],
            [(3072, 256), (0, 512), (512, 512), (1024, 512), (2048, 512), (1536, 512), (2560, 512)],
        ]
        HALO_L3_PIECES = [(1024, 512), (2048, 512), (1536, 512), (2560, 512)]
        schedule = []
        for l in range(4):
            schedule += [(0, l), (1, l)]
        schedule += [(ti, l) for ti in range(2, len(tiles)) for l in range(4)]
        for (ti_, l) in schedule:
            if True:
                t0, W, is_halo = tiles[ti_]
                if is_halo and l == 3:
                    for (c0, cols) in HALO_L3_PIECES:
                        piece_win(win_s[l], c0, cols)
                    continue
                for (c0, cols) in WIN_PIECES[l]:
                    piece_win(win_s[l], c0, cols)
                wo_l = wo_s[l * 1280:(l + 1) * 1280, :].rearrange("(k p) o -> p k o", p=128)
                for i in range(2):
                    pieces.append((wo_l[:, :, i * 512:(i + 1) * 512], v3(10, 512)))
                fin_l = fin_s[l * D:(l + 1) * D, :].rearrange("(k p) o -> p k o", p=128)
                for i in range(11):
                    pieces.append(([fin_l[:, :, i * 256:(i + 1) * 256], fin_l[:, :, DFF + i * 256:DFF + (i + 1) * 256]],
                                   lambda s: s[:, 0:4096].rearrange("p (h k o) -> p h k o", k=8, h=2)))
                fdn_l = fdn_s[l * DFF:(l + 1) * DFF, :].rearrange("(k p) o -> p k o", p=128)
                for i in range(4):
                    pieces.append((fdn_l[:, :, i * 256:(i + 1) * 256], v3(22, 256)))

        ws = {"next": 0, "issued": 0, "held": []}

        def wget(keep=False):
            if cen_jobs or cen_prev[0] is not None:
                cen_tick()
            i = ws["next"]
            ws["next"] = i + 1
            if not keep:
                ws["held"] = []
            ws["held"].append(i)
            lim = min(len(pieces), min(ws["held"]) + NSLOT)
            while ws["issued"] < lim:
                j = ws["issued"]
                src, vf = pieces[j]
                if isinstance(src, list):
                    for hh, s_ in enumerate(src):
                        P.dma("sp", vf(slots[j % NSLOT])[:, hh], s_, f"w{j % NSLOT}" + ("h" if hh else ""))
                else:
                    P.dma("sp", vf(slots[j % NSLOT]), src, f"w{j % NSLOT}")
                ws["issued"] = j + 1
            assert ws["issued"] > i
            return pieces[i][1](slots[i % NSLOT])

        for l in range(4):
            w = wget()
            for hc in range(2):
                bk = bank()
                for k in range(8):
                    P.mm(bk[:, 0:256], w[:, k, hc * 128:(hc + 1) * 128], memTb[:, k, :], k == 0, k == 7)
                P.copy("act", memK[:, l, hc, :], bk[:, 0:256])
            for mc in range(2):
                bk = bank()
                for k in range(8):
                    P.mm(bk[:, 0:256], memTb[:, k, mc * 128:(mc + 1) * 128], w[:, k, 256:512], k == 0, k == 7)
                src = bk[:, 0:256].rearrange("p (hc q d) -> p hc q d", hc=2, q=2)
                for q in range(2):
                    P.copy("dve", memV[:, l, mc, q::2, q * 64:(q + 1) * 64], src[:, :, q, :])

        class LNStats:
            def __init__(self, W, delay=1, centered=False):
                self.W = W
                self.pend = []
                self.delay = delay
                self.n = 0
                self.centered = centered
                P.act(sm(1), pcol("eps"), AF.Ln)

            def push(self, c):
                W = self.W
                a = None
                if not self.centered:
                    a = tb(W)
                    P.act(a, zT[:, c, 0:W], AF.Copy)
                q = tb(W)
                P.act(q, zT[:, c, 0:W], AF.Square)
                self.pend.append((a, q))
                if len(self.pend) > self.delay:
                    self._mm()

            def _mm(self):
                W = self.W
                a, q = self.pend.pop(0)
                if a is not None:
                    P.mm(ps[4][:, 0:W], onesb[:], a, self.n == 0, self.n == 7)
                P.mm(ps[5][:, 0:W], onesb[:], q, self.n == 0, self.n == 7)
                self.n += 1

            def flush(self):
                while self.pend:
                    self._mm()
                assert self.n == 8

        def layer_norm(W, g_name, b_name, mode, out_c=None, stats=None, inplace_out=False):
            S1, S2, R = ps[4], ps[5], ps[6]
            if stats is None:
                stats = LNStats(W)
                for c in range(8):
                    stats.push(c)
            stats.flush()
            centered = stats.centered
            lv = tf(W)
            if centered:
                P.act(lv, S2[:, 0:W], AF.Ln, scale=1.0 / D, bias=pcol("eps"))
            else:
                msq = tf(W)
                P.act(msq, S1[:, 0:W], AF.Square, scale=1.0 / D)
                var = tf(W)
                P.stt(var, S2[:, 0:W], 1.0 / D, msq, ALU.mult, ALU.subtract)
                P.act(lv, var, AF.Ln, bias=pcol("eps"))
            P.act(R[:, 0:W], lv, AF.Exp, scale=-0.5)
            deferred = []

            def fin(c):
                z = zT[:, c, 0:W]
                P.tt("dve", z, R[:, 0:W], z, ALU.mult)
                g_, b_ = pcol(g_name, c), pcol(b_name, c)
                if mode == "silu":
                    P.act(out_c(c), z, AF.Silu, scale=g_, bias=b_)
                    return
                if mode == "res":
                    P.act(xb[:, c, 0:W], z, AF.Identity, scale=g_, bias=b_)
                dst = z if inplace_out else xT[:, c, 0:W]
                deferred.append(lambda: P.act(dst, z, AF.Identity, scale=g_, bias=b_))

            if centered:
                for c in range(8):
                    fin(c)
                return deferred
            for c in range(8):
                P.stt(zT[:, c, 0:W], S1[:, 0:W], -1.0 / D, zT[:, c, 0:W], ALU.mult, ALU.add)
                if c >= 1:
                    fin(c - 1)
            fin(7)
            return deferred

        def compute_mx(W):
            bk = bank()
            for k in range(8):
                P.mm(bk[:, 0:W], onesb[:], xb[:, k, 0:W], k == 0, k == 7)
            P.act(mxr[0:1, 0:W], bk[0:1, 0:W], AF.Identity, scale=ALPHA / D)

        def flush(deferred, n=None):
            k = len(deferred) if n is None else min(n, len(deferred))
            for _ in range(k):
                deferred.pop(0)()

        PXv = RB[:, 9216:9216 + 2048].rearrange("p (a b w) -> p a b w", a=2, b=2)

        def xattn_A(l, W, qxT, hc):
            for q in range(2):
                for mc in range(2):
                    bk = ps[4 + q * 2 + mc]
                    P.mm(bk[:, 0:W], memK[q * 64:(q + 1) * 64, l, hc, mc * 128:(mc + 1) * 128],
                         qxT[q * 64:(q + 1) * 64, hc, 0:W])
                    P.act(PXv[:, q, mc, 0:W], bk[:, 0:W], AF.Exp, scale=0.125)

        def xattn_B(l, W, hc):
            O, Dn = bank(), bank()
            n = 0
            for q in range(2):
                for mc in range(2):
                    P.mm(O[:, 0:W], memV[:, l, mc, 2 * hc + q, :], PXv[:, q, mc, 0:W], n == 0, n == 3)
                    n += 1
            n = 0
            for q in range(2):
                for mc in range(2):
                    P.mm(Dn[:, 0:W], onesp[:, q, :], PXv[:, q, mc, 0:W], n == 0, n == 3)
                    n += 1
            lr = tf(W)
            P.act(lr, Dn[:, 0:W], AF.Ln)
            r = tf(W)
            P.act(r, lr, AF.Exp, scale=-1.0)
            P.tt("dve", brT[:, 8 + hc, 0:W], O[:, 0:W], r, ALU.mult)

        def evac_qx(w, col0, W, qxT):
            for hc in range(2):
                bk = bank()
                for k in range(8):
                    P.mm(bk[:, 0:W], w[:, k, col0 + hc * 128:col0 + (hc + 1) * 128], xb[:, k, 0:W], k == 0, k == 7)
                P.copy("act", qxT[:, hc, 0:W], bk[:, 0:W])

        def mm_chunk(w, col, W, bk=None):
            bk = bk or bank()
            for k in range(8):
                P.mm(bk[:, 0:W], w[:, k, col:col + 128], xb[:, k, 0:W], k == 0, k == 7)
            return bk

        def qxT_view():
            return RB[:, 8192:9216].rearrange("p (c w) -> p c w", c=2)

        def mixer0(W, first_main, deferred):
            nb = W // 128
            u = RF[:, 0:4096].rearrange("p (c w) -> p c w", c=8)
            vg = RF[:, 4096:6144].rearrange("p (a f) -> p a f", a=2)
            nrm = RB[:, 0:4096].rearrange("p (b f) -> p b f", b=4)
            qxT = qxT_view()
            w = wget()
            evac_qx(w, 0, W, qxT)
            flush(deferred)
            xattn_A(0, W, qxT, 0)
            for pc in range(2):
                w = wget()
                for j in range(4):
                    c = pc * 4 + j
                    bk = mm_chunk(w, j * 128, W)
                    P.act(u[:, c, 0:W], bk[:, 0:W], AF.Gelu_apprx_tanh)
                if pc == 0:
                    xattn_B(0, W, 0)
                    xattn_A(0, W, qxT, 1)
                else:
                    xattn_B(0, W, 1)
            wa = wget()
            wb_ = wget(keep=True)
            for b in range(nb):
                vb = vg[:, b % 2, :]
                for hf, w in enumerate((wa, wb_)):
                    bk = bank()
                    for k in range(8):
                        P.mm(bk[:, :], xb[:, k, b * 128:(b + 1) * 128], w[:, k, :], k == 0, k == 7)
                    P.act(vb[:, hf * 512:(hf + 1) * 512], bk[:, :], AF.Gelu_apprx_tanh)
                st = sm(12)
                P.add("dve", lambda h, o=st[:, 0:6], i=vb[:, 0:512]: h.bn_stats(o, i), [vb[:, 0:512]], [st[:, 0:6]])
                P.add("dve", lambda h, o=st[:, 6:12], i=vb[:, 512:1024]: h.bn_stats(o, i), [vb[:, 512:1024]], [st[:, 6:12]])
                mv = sm(2)
                P.add("dve", lambda h, o=mv, i=st: h.bn_aggr(o, i), [st], [mv])
                sd = sm(1)
                P.act(sd, mv[:, 1:2], AF.Sqrt, bias=pcol("eps"))
                rs = sm(1)
                P.recip(rs, sd)
                nbias = sm(1)
                P.ts("dve", nbias, mv[:, 0:1], -1.0, rs, ALU.mult, ALU.mult)
                P.act(nrm[:, b, :], vb, AF.Identity, scale=rs, bias=nbias)
            for g in range(8):
                bk = bank()
                for b in range(nb):
                    P.mm(bk[:, b * 128:(b + 1) * 128], nrm[:, b, g * 128:(g + 1) * 128], wmT[:, g, :])
                sv = tf(W)
                P.stt(sv.rearrange("p (b i) -> p b i", i=128), bk[:, 0:W].rearrange("p (b i) -> p b i", i=128),
                      pcol("avln_g", g), Cg[:, g:g + 1, :].to_broadcast([128, nb, 128]), ALU.mult, ALU.add)
                P.tt("dve", brT[:, g, 0:W], u[:, g, 0:W], sv, ALU.mult)

        def mixer1(W, first_main, deferred):
            nb = W // 128
            Qpad = RB[:, 0:8192].rearrange("p (m q w) -> p m q w", m=8, q=2)
            qxT = qxT_view()
            P.memset(PL[0], Qpad[64:128, :, 0, :], 0.0)
            P.memset(PL[0], Qpad[0:64, :, 1, :], 0.0)
            w = wget()
            evac_qx(w, 256, W, qxT)
            flush(deferred)
            xattn_A(1, W, qxT, 0)
            for kh in range(2):
                bk = bank()
                for half in range(2):
                    for k in range(8):
                        P.mm(bk[half * 64:(half + 1) * 64, 0:W], w[:, k, kh * 64:(kh + 1) * 64], xb[:, k, 0:W],
                             k == 0, k == 7)
                P.copy("act", Kdup[:, kh, 128:128 + W], bk[:, 0:W])
            bk = bank()
            for b in range(nb):
                for k in range(8):
                    P.mm(bk[:, b * 128:(b + 1) * 128], xb[:, k, b * 128:(b + 1) * 128], w[:, k, 128:256], k == 0, k == 7)
            src = bk[:, 0:W].rearrange("p (b kh d) -> p b kh d", kh=2, d=64)
            for kh in range(2):
                for q in range(2):
                    P.copy("dve" if q else "act", Vpad[:, 1:1 + nb, kh, q, q * 64:(q + 1) * 64], src[:, :, kh, :])
            for pc in range(2):
                w = wget()
                for j in range(4):
                    m = pc * 4 + j
                    bk = mm_chunk(w, j * 128, W)
                    P.act(Qpad[0:64, m, 0, 0:W], bk[0:64, 0:W], AF.Identity, scale=0.125)
                    P.ts("dve", Qpad[64:128, m, 1, 0:W], bk[64:128, 0:W], 0.125, None, ALU.mult)
                if pc == 0:
                    xattn_B(1, W, 0)
                    xattn_A(1, W, qxT, 1)
                else:
                    xattn_B(1, W, 1)
            PTbuf = RB[:, 9216:11264]
            for qb in range(nb):
                for kh in range(2):
                    pts = []
                    for bi in range(2):
                        kcols = slice((qb + bi) * 128, (qb + bi + 1) * 128)
                        pt = PTbuf[:, bi * 1024:(bi + 1) * 1024]
                        for hf in range(2):
                            bk = ps[bi * 2 + hf]
                            h0 = kh * 8 + hf * 4
                            P.mm(bk[:, :], identb[:], biasTb[:, bi, h0:h0 + 4, :].rearrange("p h q -> p (h q)"), True, False)
                            for mm_ in range(2):
                                m = hf * 2 + mm_
                                P.mm(bk[:, mm_ * 256:(mm_ + 1) * 256].rearrange("p (q w) -> p q w", q=2),
                                     Kdup[:, kh, kcols], Qpad[:, kh * 4 + m, :, qb * 128:(qb + 1) * 128],
                                     False, mm_ == 1)
                            P.act(pt[:, hf * 512:(hf + 1) * 512], bk[:, :], AF.Exp)
                        if first_main and qb == 0 and bi == 0:
                            P.ts("dve", pt, pt, flag[:, 0:1], None, ALU.mult)
                        pts.append(pt)
                    O, Dn = ps[4], ps[5]
                    n = 0
                    for bi in range(2):
                        pv = pts[bi].rearrange("p (m q w) -> p m q w", m=4, q=2)
                        for q in range(2):
                            P.mm(O[:, :].rearrange("p (m w) -> p m w", m=4), Vpad[:, qb + bi, kh, q, :], pv[:, :, q, :],
                                 n == 0, n == 3)
                            n += 1
                    n = 0
                    for bi in range(2):
                        pv = pts[bi].rearrange("p (m q w) -> p m q w", m=4, q=2)
                        for q in range(2):
                            P.mm(Dn[:, :].rearrange("p (m w) -> p m w", m=4), onesp[:, q, :], pv[:, :, q, :],
                                 n == 0, n == 3)
                            n += 1
                    dsum = tf(512)
                    P.tt("dve", dsum.rearrange("p (m w) -> p m w", m=4), Dn[:, :].rearrange("p (m w) -> p m w", m=4),
                         esk[:, kh * 4:(kh + 1) * 4].unsqueeze(2).to_broadcast([128, 4, 128]), ALU.add)
                    lr = tf(512)
                    P.act(lr, dsum, AF.Ln)
                    r = tf(512)
                    P.act(r, lr, AF.Exp, scale=-1.0)
                    P.tt("dve", brT[:, kh * 4:(kh + 1) * 4, qb * 128:(qb + 1) * 128],
                         O[:, :].rearrange("p (m w) -> p m w", m=4), r.rearrange("p (m w) -> p m w", m=4), ALU.mult)
            P.copy(PL[0], Kdup[:, :, 0:128], Kdup[:, :, W:W + 128])
            P.copy(PL[0], Vpad[:, 0], Vpad[:, nb])

        def mixer2(W, first_main, deferred):
            glu = RB[:, 0:8 * 544].rearrange("p (c w) -> p c w", c=8)
            qxT = qxT_view()
            P.copy(PL[0], glu[:, :, 0:32], gcar[:])
            if first_main:
                P.ts("dve", glu[:, :, 0:32], glu[:, :, 0:32], flag[:, 0:1], None, ALU.mult)
            w = wget()
            evac_qx(w, 0, W, qxT)
            flush(deferred)
            xattn_A(2, W, qxT, 0)
            for pc in range(2):
                wa = wget()
                wg = wget(keep=True)
                for j in range(4):
                    c = pc * 4 + j
                    ba = mm_chunk(wa, j * 128, W)
                    bg_ = mm_chunk(wg, j * 128, W)
                    sg = tf(W)
                    P.act(sg, bg_[:, 0:W], AF.Sigmoid)
                    P.tt("dve", glu[:, c, 32:32 + W], ba[:, 0:W], sg, ALU.mult)
                if pc == 0:
                    xattn_B(2, W, 0)
                    xattn_A(2, W, qxT, 1)
                else:
                    xattn_B(2, W, 1)
            st2 = LNStats(W)
            for c in range(8):
                dgc = DG[c % 2]
                o = POFF["cconv_w"] + c * 31
                P.tt(PL[0], dgc, identb[:].unsqueeze(1).to_broadcast([128, 31, 128]),
                     par[:, o:o + 31].unsqueeze(2).to_broadcast([128, 31, 128]), ALU.mult)
                bk = bank()
                for k in range(31):
                    P.mm(bk[:, 0:W], dgc[:, k, :], glu[:, c, 2 + k:2 + k + W], k == 0, k == 30)
                P.act(zT[:, c, 0:W], bk[:, 0:W], AF.Identity, bias=pcol("cconv_b", c))
                st2.push(c)
            P.copy(PL[0], gcar[:], glu[:, :, W:W + 32])
            layer_norm(W, "cln_g", "cln_b", "silu", out_c=lambda c: brT[:, c, 0:W], stats=st2)

        def mixer3(W, first_main, deferred, halo=False):
            bgs = RF[:, 0:4096].rearrange("p (c w) -> p c w", c=8)
            pb = RB[:, 0:8 * 514].rearrange("p (c w) -> p c w", c=8)
            qxT = qxT_view()
            P.copy(PL[0], pb[:, :, 0:2], pcar[:])
            if first_main:
                P.ts("dve", pb[:, :, 0:2], pb[:, :, 0:2], flag[:, 0:1], None, ALU.mult)
            if not halo:
                for c in range(8):
                    o = POFF["dconv_w"] + c * 3
                    P.tt(PL[0], diag3[:, c], identb[:].unsqueeze(1).to_broadcast([128, 3, 128]),
                         par[:, o:o + 3].unsqueeze(2).to_broadcast([128, 3, 128]), ALU.mult)
                w = wget()
                evac_qx(w, 0, W, qxT)
                flush(deferred)
                xattn_A(3, W, qxT, 0)
                for pc in range(2):
                    w = wget()
                    for j in range(4):
                        c = pc * 4 + j
                        bk = mm_chunk(w, j * 128, W)
                        P.copy("act", bgs[:, c, 0:W], bk[:, 0:W])
                    if pc == 0:
                        xattn_B(3, W, 0)
                        xattn_A(3, W, qxT, 1)
                    else:
                        xattn_B(3, W, 1)
            else:
                flush(deferred)
            for pc in range(2):
                wc = wget()
                wh = wget(keep=True)
                for j in range(4):
                    c = pc * 4 + j
                    bc = mm_chunk(wc, j * 128, W)
                    bh = mm_chunk(wh, j * 128, W)
                    cs = tf(W)
                    P.copy("act", cs, bc[:, 0:W])
                    P.tt("dve", pb[:, c, 2:2 + W], bh[:, 0:W], cs, ALU.mult)
            P.copy(PL[0], pcar[:], pb[:, :, W:W + 2])
            if halo:
                return
            for c in range(8):
                bk = bank()
                for k in range(3):
                    P.mm(bk[:, 0:W], diag3[:, c, k, :], pb[:, c, k:k + W], k == 0, k == 2)
                P.tt("dve", brT[:, c, 0:W], bk[:, 0:W], bgs[:, c, 0:W], ALU.mult)

        mixers = [mixer0, mixer1, mixer2, mixer3]

        out_tokens = []
        xT_dv = xT_d.rearrange("(c p) t -> p c t", p=128)
        outT_dv = outT_d.rearrange("(c p) t -> p c t", p=128)

        def load_x_dma(ti):
            t0, W, _ = tiles[ti]
            P.dma("sp", xT[:, :, 0:W], xT_dv[:, :, t0:t0 + W], "xin_h" if ti == 0 else "xin")

        def load_x_cast(ti):
            t0, W, _ = tiles[ti]
            for c in range(8):
                P.copy(("dve" if ti <= 1 else "pool") if c % 2 else "act", xb[:, c, 0:W], xT[:, c, 0:W])

        defer_t = {}
        PL = ["pool"]

        def run_layer(ti, l):
            t0, W, is_halo = tiles[ti]
            deferred = defer_t.get(ti, [])
            if is_halo and l == 3:
                mixer3(W, False, deferred, halo=True)
                return
            mixers[l](W, ti == 1, deferred)
            assert not deferred
            st1 = LNStats(W, centered=True)
            compute_mx(W)
            for pc in range(2):
                w = wget()
                for j in range(4):
                    oc = pc * 4 + j
                    bk = bank()
                    for k in range(10):
                        P.mm(bk[:, 0:W], w[:, k, j * 128:(j + 1) * 128], brT[:, k, 0:W], k == 0, False)
                    P.mm(bk[:, 0:W], negone[0:1, :], mxr[0:1, 0:W], False, True)
                    P.stt(zT[:, oc, 0:W], xT[:, oc, 0:W], ALPHA, bk[:, 0:W], ALU.mult, ALU.add)
                    st1.push(oc)
            deferred = layer_norm(W, f"ln1_g{l}", f"ln1_b{l}", "res", stats=st1)
            prefetch = (l == 3) and (ti + 1 < len(tiles))
            XP = RF[:, 0:4096].rearrange("p (c w) -> p c w", c=8)
            if prefetch:
                tn = tiles[ti + 1][0]
                P.dma("sp", XP, xT_dv[:, :, tn:tn + TW], "xin")
            gT = RB[:, 0:22 * TW].rearrange("p (c w) -> p c w", c=22)
            for i in range(11):
                w = wget()
                for j in range(2):
                    ch = 2 * i + j
                    b1 = bank()
                    for k in range(8):
                        P.mm(b1[:, 0:W], w[:, 0, k, j * 128:(j + 1) * 128], xb[:, k, 0:W], k == 0, k == 7)
                    b2 = bank()
                    for k in range(8):
                        P.mm(b2[:, 0:W], w[:, 1, k, j * 128:(j + 1) * 128], xb[:, k, 0:W], k == 0, k == 7)
                    s1 = tf(W)
                    P.act(s1, b1[:, 0:W], AF.Silu)
                    P.tt("dve", gT[:, ch, 0:W], b2[:, 0:W], s1, ALU.mult)
                    flush(deferred, 1)
                if i == 1:
                    compute_mx(W)
            assert not deferred
            if prefetch:
                for c in range(8):
                    P.copy("act", xb[:, c, :], XP[:, c, :])
            st3 = LNStats(W, centered=True)
            for pc in range(4):
                w = wget()
                for j in range(2):
                    oc = pc * 2 + j
                    bk = bank()
                    for k in range(22):
                        P.mm(bk[:, 0:W], w[:, k, j * 128:(j + 1) * 128], gT[:, k, 0:W], k == 0, False)
                    P.mm(bk[:, 0:W], negone[0:1, :], mxr[0:1, 0:W], False, True)
                    P.stt(zT[:, oc, 0:W], xT[:, oc, 0:W], ALPHA, bk[:, 0:W], ALU.mult, ALU.add)
                    st3.push(oc)
            if l < 3:
                defer_t[ti] = layer_norm(W, f"ln2_g{l}", f"ln2_b{l}", "res", stats=st3)
                if ti <= 1:
                    flush(defer_t[ti])
            else:
                if prefetch:
                    P.dma("sp", xT[:, :, :], XP, "xcp")
                deferred = layer_norm(W, f"ln2_g{l}", f"ln2_b{l}", "f32only", stats=st3, inplace_out=True)

                def store(t0=t0, W=W):
                    out_tokens.append(P.dma("sp", outT_dv[:, :, t0 - HALO:t0 - HALO + W], zT[:, :, 0:W], "out"))

                deferred.append(store)
                defer_t[ti] = []
                if prefetch:
                    defer_t[ti + 1] = deferred
                else:
                    flush(deferred)

        xT_m, xb_m = xT, xb
        xT, xb = xT_h, xb_h
        load_x_dma(0)
        load_x_cast(0)
        xT, xb = xT_m, xb_m
        load_x_dma(1)
        load_x_cast(1)
        for (ti, l) in schedule:
            if ti == 0:
                xT, xb = xT_h, xb_h
                if l + 1 < 4 and LAZY:
                    convert_layer(l + 1)
                    convert_centered_layer(l + 1)
            else:
                xT, xb = xT_m, xb_m
            if ti == 1 and l == 3 and len(tiles) == 2:
                pass
            PL[0] = "dve" if (ti <= 1 and LAZY) else "pool"
            run_layer(ti, l)
            if ti == 1:
                cen_flush()
        P.add("sp", None, extra_deps=out_tokens[-1:])
        assert ws["next"] == len(pieces), (ws["next"], len(pieces))

        engsem = {e: es.enter_context(nc.semaphore(f"sem_{e}")) for e in Prog.ENGS}
        dmasem = {k: es.enter_context(nc.semaphore(f"dsem_{k}")) for k in P.dma_cnt}
        with nc.Block() as block:
            P.emit(block, engsem, dmasem)
    return nc


def _host_prep(inp, n_main=NMAIN, cores=None):
    x = np.asarray(inp["x"], np.float32)
    mem = np.asarray(inp["mem"], np.float32)
    B, S, _ = x.shape
    per = n_main * TW
    par = np.zeros((128, NPAR), np.float32)

    def put(name, arr):
        arr = np.asarray(arr, np.float32)
        par[:, POFF[name]:POFF[name] + arr.shape[1]] = arr

    for l in range(4):
        put(f"ln1_g{l}", _fm(inp["ln1_g"][l]))
        put(f"ln1_b{l}", _fm(inp["ln1_b"][l]))
        put(f"ln2_g{l}", _fm(inp["ln2_g"][l]))
        put(f"ln2_b{l}", _fm(inp["ln2_b"][l]))
    put("avln_g", _fm(inp["a_v_ln_g"][0]))
    put("avln_b", _fm(inp["a_v_ln_b"][0]))
    put("cln_g", _fm(inp["c_ln_g"][0]))
    put("cln_b", _fm(inp["c_ln_b"][0]))
    put("cconv_b", _fm(inp["c_conv_b"][0]))
    sinks = np.asarray(inp["b_sinks"], np.float32)[0]
    srep = np.zeros((128, 8), np.float32)
    for kh in range(2):
        for m in range(4):
            srep[0:64, kh * 4 + m] = sinks[kh * 8 + 2 * m]
            srep[64:128, kh * 4 + m] = sinks[kh * 8 + 2 * m + 1]
    put("sinks", srep)
    cw = np.asarray(inp["c_conv_w"], np.float32)[0]
    put("cconv_w", cw.T.reshape(8, 128, 31).transpose(1, 0, 2).reshape(128, 8 * 31))
    dw = np.asarray(inp["d_conv_w"], np.float32)[0]
    put("dconv_w", dw.T.reshape(8, 128, 3).transpose(1, 0, 2).reshape(128, 24))
    par[:, POFF["eps"]] = EPS

    bs = np.asarray(inp["a_b_s"], np.float32)[0]
    bs_rep = np.ascontiguousarray(np.broadcast_to(bs.reshape(1, 8 * 128), (128, 8 * 128)))
    ws = np.asarray(inp["a_w_s"], np.float32)[0]
    wsT = np.ascontiguousarray(ws.transpose(2, 0, 1).reshape(128, 8 * 128))
    rb = np.asarray(inp["rel_bias"], np.float32)
    q = np.arange(128)[None, :]
    biasT = np.zeros((128, 2, 16, 128), np.float32)
    for bi in range(2):
        kpos = (np.arange(128) + (bi - 1) * 128)[:, None]
        bucket = _t5_bucket(kpos - q)
        kc = np.floor_divide(kpos, 64)
        qc = q // 64
        valid = (kc >= qc - 2) & (kc <= qc)
        g = rb[bucket]
        g = np.where(valid[:, :, None], g, np.float32(-1e30))
        biasT[:, bi] = g.transpose(0, 2, 1)
    biasT = np.ascontiguousarray(biasT.reshape(128, 2 * 16 * 128))
    shared = {
        "params": par, "bs_rep": bs_rep, "wsT": wsT, "biasT": biasT,
        "ident": np.eye(128, dtype=np.float32),
        "a_w_in": np.ascontiguousarray(inp["a_w_in"][0], np.float32),
        "b_w_in": np.ascontiguousarray(inp["b_w_in"][0], np.float32),
        "c_w_in": np.ascontiguousarray(inp["c_w_in"][0], np.float32),
        "d_w_in": np.ascontiguousarray(inp["d_w_in"][0], np.float32),
        "w_mem_kv": np.ascontiguousarray(np.asarray(inp["w_mem_kv"], np.float32).reshape(4 * D, 512)),
        "w_o": np.ascontiguousarray(np.asarray(inp["w_o"], np.float32).reshape(4 * 1280, D)),
        "ffn_w_in": np.ascontiguousarray(np.asarray(inp["ffn_w_in"], np.float32).reshape(4 * D, 2 * DFF)),
        "ffn_w_down": np.ascontiguousarray(np.asarray(inp["ffn_w_down"], np.float32).reshape(4 * DFF, D)),
    }
    if cores is None:
        cores = [(b, s0) for b in range(B) for s0 in range(0, S, per)]
    in_maps = []
    for (b, s0) in cores:
        xe = np.zeros((HALO + per, D), np.float32)
        if s0 > 0:
            xe[:] = x[b, s0 - HALO:s0 + per]
            fl = 1.0
        else:
            xe[HALO:] = x[b, 0:per]
            fl = 0.0
        m = dict(shared)
        m["xT"] = np.ascontiguousarray(xe.T)
        m["memT"] = np.ascontiguousarray(mem[b].T)
        m["flag"] = np.full((128, 1), fl, np.float32)
        in_maps.append(m)
    return in_maps, cores


def kernel(**inputs):
    in_maps, cores = _host_prep(inputs)
    nc = build(NMAIN)
    res = run_bass_kernel_spmd(nc, in_maps, core_ids=list(range(NCORES)))
    x = inputs["x"]
    B, S, _ = x.shape
    out = np.empty((B, S, D), np.float32)
    per = NMAIN * TW
    for (b, s0), r in zip(cores, res.results):
        out[b, s0:s0 + per] = np.asarray(r["outT"]).T
    return out
```
